# Optimizing a Trainium2 kernel written in Bass

```python
import jax, jax.numpy as jnp
from jax import lax
import numpy as np

D_MODEL = 1024
BATCH = 32
SEQ = 2048
DEPTH = 1

N_META = 16
D_CONV = D_MODEL
CONV_WIDTH = 3
GLA_HEADS = 4
DK = D_MODEL // 2
DV = D_MODEL
HEAD_K = DK // GLA_HEADS
HEAD_V = DV // GLA_HEADS
GATE_RANK = 16
GATE_NORMALIZER = 16.0
CHUNK = 64
EPS = 1e-6
IN_SPLITS = (D_CONV, D_CONV, D_CONV, D_CONV, DK, DK, DV, DV, GATE_RANK, GATE_RANK, D_MODEL, D_MODEL)
N_IN = sum(IN_SPLITS)

kernel_name = "hybrid_gated_shortconv_bigla_block"


def rms_norm(x, g):
    xf = x.astype(jnp.float32)
    y = xf * lax.rsqrt(jnp.mean(xf * xf, axis=-1, keepdims=True) + EPS)
    return (y * g.astype(jnp.float32)).astype(x.dtype)


def short_conv_centred(u, w):
    half = CONV_WIDTH // 2
    L = u.shape[1]
    up = jnp.pad(u, ((0, 0), (half, half), (0, 0)))
    return sum(up[:, i:i + L] * w[i] for i in range(CONV_WIDTH))


def to_chunks(t, pad_front, pad_back, n_heads, head_dim):
    t = jnp.pad(t, ((0, 0), (pad_front, pad_back), (0, 0)))
    bn, lp, _ = t.shape
    t = t.reshape(bn, lp // CHUNK, CHUNK, n_heads, head_dim)
    return t.transpose(0, 3, 1, 2, 4)


def gla_chunked(q, k, v, g, strict):
    bn, nh, _, c, dk = q.shape
    dv = v.shape[-1]
    b = jnp.cumsum(g.astype(jnp.float32), axis=3)
    q_in = q * jnp.exp(b)
    k_in = k * jnp.exp(-b)
    scores = jnp.einsum('bhncd,bhnjd->bhncj', q_in, k_in)
    mask = jnp.tril(jnp.ones((c, c), dtype=bool), k=-1 if strict else 0)
    scores = jnp.where(mask, scores, 0.0)
    o_intra = jnp.einsum('bhncj,bhnje->bhnce', scores, v)
    b_last = b[..., -1:, :]
    k_dec = k * jnp.exp(b_last - b)
    decay = jnp.exp(b_last[..., 0, :])

    def step(state, xs):
        q_n, k_n, v_n, d_n = xs
        o_n = jnp.einsum('bhcd,bhde->bhce', q_n, state)
        state = state * d_n[..., None] + jnp.einsum('bhcd,bhce->bhde', k_n, v_n)
        return state, o_n

    xs = tuple(jnp.moveaxis(t, 2, 0) for t in (q_in, k_dec, v, decay))
    s0 = jnp.zeros((bn, nh, dk, dv), jnp.float32)
    _, o_inter = lax.scan(step, s0, xs)
    return o_intra + jnp.moveaxis(o_inter, 0, 2)


def hybrid_layer(h, g_pre, w_in, conv_w, w_gate_f, b_gate_f, w_gate_b, b_gate_b,
                 gla_g, w_out_c, w_out_g, w_out, g_post):
    bn, L, _ = h.shape
    pad_front = (-N_META) % CHUNK
    pad_back = (-(L - N_META)) % CHUNK
    u = rms_norm(h, g_pre)
    proj = jnp.einsum('bld,dn->bln', u, w_in)
    split_idx = np.cumsum(IN_SPLITS)[:-1].tolist()
    (c_b, c_c, c_x, c_z, q, k, v, r, lr_f, lr_b, m_a, m_b) = jnp.split(proj, split_idx, axis=-1)

    y_conv = c_b * short_conv_centred(c_c * c_x, conv_w) * jax.nn.silu(c_z)
    p_conv = jnp.einsum('blc,cd->bld', y_conv, w_out_c)

    g_f = jax.nn.log_sigmoid((lr_f @ w_gate_f + b_gate_f).astype(jnp.float32)) / GATE_NORMALIZER
    g_b = jax.nn.log_sigmoid((lr_b @ w_gate_b + b_gate_b).astype(jnp.float32)) / GATE_NORMALIZER
    qc = to_chunks(q * (HEAD_K ** -0.5), pad_front, pad_back, GLA_HEADS, HEAD_K)
    kc = to_chunks(k, pad_front, pad_back, GLA_HEADS, HEAD_K)
    vc = to_chunks(v, pad_front, pad_back, GLA_HEADS, HEAD_V)
    gfc = to_chunks(g_f, pad_front, pad_back, GLA_HEADS, HEAD_K)
    gbc = to_chunks(g_b, pad_front, pad_back, GLA_HEADS, HEAD_K)
    rev = lambda t: jnp.flip(t, axis=(2, 3))
    o_f = gla_chunked(qc, kc, vc, gfc, strict=False)
    o_b = rev(gla_chunked(rev(qc), rev(kc), rev(vc), rev(gbc), strict=True))
    o = (o_f + o_b).transpose(0, 2, 3, 1, 4).reshape(bn, -1, GLA_HEADS, HEAD_V)
    o = o[:, pad_front:pad_front + L]
    o = rms_norm(o, gla_g).reshape(bn, L, DV).astype(h.dtype)
    y_gla = o * jax.nn.silu(r)
    p_gla = jnp.einsum('blc,cd->bld', y_gla, w_out_g)

    merged = jax.nn.sigmoid(m_a) * p_conv + jax.nn.sigmoid(m_b) * p_gla
    out = jnp.einsum('bld,de->ble', merged, w_out)
    return h + rms_norm(out, g_post)


def setup_inputs(seed: int = 0) -> dict:
    key = jax.random.key(seed)
    ks = jax.random.split(key, 16)
    nrm = lambda k, shape, scale: jax.random.normal(k, shape, jnp.float32) * scale
    return {
        "x": nrm(ks[0], (BATCH, SEQ, D_MODEL), 1.0),
        "meta_tokens": nrm(ks[1], (N_META, D_MODEL), 1.0),
        "norm_pre": 1.0 + nrm(ks[2], (DEPTH, D_MODEL), 0.05),
        "w_in": nrm(ks[3], (DEPTH, D_MODEL, N_IN), D_MODEL ** -0.5),
        "conv_w": nrm(ks[4], (DEPTH, CONV_WIDTH, D_CONV), CONV_WIDTH ** -0.5),
        "w_gate_fwd": nrm(ks[5], (DEPTH, GATE_RANK, DK), GATE_RANK ** -0.5),
        "b_gate_fwd": nrm(ks[6], (DEPTH, DK), 0.1),
        "w_gate_bwd": nrm(ks[7], (DEPTH, GATE_RANK, DK), GATE_RANK ** -0.5),
        "b_gate_bwd": nrm(ks[8], (DEPTH, DK), 0.1),
        "gla_norm": 1.0 + nrm(ks[9], (DEPTH, HEAD_V), 0.05),
        "w_out_conv": nrm(ks[10], (DEPTH, D_CONV, D_MODEL), D_CONV ** -0.5),
        "w_out_gla": nrm(ks[11], (DEPTH, DV, D_MODEL), DV ** -0.5),
        "w_merge_out": nrm(ks[12], (DEPTH, D_MODEL, D_MODEL), D_MODEL ** -0.5),
        "norm_post": 1.0 + nrm(ks[13], (DEPTH, D_MODEL), 0.05),
    }


def reference(x, meta_tokens, norm_pre, w_in, conv_w, w_gate_fwd, b_gate_fwd, w_gate_bwd,
              b_gate_bwd, gla_norm, w_out_conv, w_out_gla, w_merge_out, norm_post):
    bn = x.shape[0]
    meta = jnp.broadcast_to(meta_tokens[None].astype(x.dtype), (bn, N_META, D_MODEL))
    h = jnp.concatenate([meta, x], axis=1)
    for l in range(DEPTH):
        h = hybrid_layer(h, norm_pre[l], w_in[l], conv_w[l], w_gate_fwd[l], b_gate_fwd[l],
                         w_gate_bwd[l], b_gate_bwd[l], gla_norm[l], w_out_conv[l],
                         w_out_gla[l], w_merge_out[l], norm_post[l])
    return h[:, N_META:]
```

```python
import numpy as np
import ml_dtypes
from contextlib import ExitStack

import concourse.bass as bass
import concourse.mybir as mybir
from concourse.bass_utils import run_bass_kernel_spmd

F32 = mybir.dt.float32
BF16 = mybir.dt.bfloat16
F32R = mybir.dt.float32r
AF = mybir.ActivationFunctionType
ALU = mybir.AluOpType

N_CORES = 8
D = 1024
SEQ = 2048
N_IN = 9248
EPS = 1e-6
C_CB, C_CC, C_CX, C_CZ = 0, 1024, 2048, 3072
C_Q, C_K, C_V, C_R, C_LR, C_MA, C_MB = 4096, 4608, 5120, 6144, 7168, 7200, 8224


class Op:
    __slots__ = ("eng", "fn", "deps", "sig", "need", "semkey", "idx")

    def __init__(self, eng, fn, semkey=None):
        self.eng = eng
        self.fn = fn
        self.deps = set()
        self.sig = None
        self.need = False
        self.semkey = semkey
        self.idx = None


class Sched:
    ENGS = ("pe", "act", "dve", "pool", "sp")

    def __init__(self):
        self.ops = {e: [] for e in self.ENGS}
        self.last_w = {}
        self.readers = {}
        self.shared_all = set()

    def add(self, eng, fn, reads=(), writes=(), semkey=None, extra=()):
        op = Op(eng, fn, semkey)
        _check_keys(reads)
        _check_keys(writes)
        deps = op.deps
        deps.update(extra)
        for k in reads:
            w = self.last_w.get(k)
            if w is not None:
                deps.add(w)
        for k in writes:
            w = self.last_w.get(k)
            if w is not None:
                deps.add(w)
            for r in self.readers.get(k, ()):
                deps.add(r)
        deps.discard(op)
        if eng == "pe":
            op.deps = {d for d in deps if not (d.eng == "pe" and d.semkey is None)}
        for k in reads:
            self.readers.setdefault(k, []).append(op)
        for k in writes:
            self.last_w[k] = op
            self.readers[k] = []
        op.idx = len(self.ops[eng])
        self.ops[eng].append(op)
        return op

    def finalize(self):
        for e in self.ENGS:
            for op in self.ops[e]:
                best = {}
                keep = set()
                for d in op.deps:
                    if d.semkey is not None:
                        keep.add(d)
                    elif d.eng not in best or d.idx > best[d.eng].idx:
                        best[d.eng] = d
                keep.update(best.values())
                op.deps = keep
                for d in op.deps:
                    d.need = True
        self.nsig = {}
        for e in self.ENGS:
            c = 0
            for op in self.ops[e]:
                if op.semkey is None and op.need:
                    c += 1
                    op.sig = (("eng", e), c)
        cnt = {}
        dma_ops = []
        for e in self.ENGS:
            for op in self.ops[e]:
                if op.semkey is not None:
                    cnt[op.semkey] = cnt.get(op.semkey, 0) + 1
                    op.sig = (("dma", op.semkey), 16 * cnt[op.semkey])
                    dma_ops.append(op)
        for op in dma_ops:
            if op.semkey in self.shared_all:
                op.sig = (("dma", op.semkey), 16 * cnt[op.semkey])
        self.dma_keys = list(cnt.keys())

    def emit(self, eng, e, sems):
        waited = {}
        for op in self.ops[eng]:
            need = {}
            for d in op.deps:
                s, v = d.sig
                if v > need.get(s, 0):
                    need[s] = v
            for s, v in need.items():
                if waited.get(s, 0) >= v:
                    continue
                e.wait_ge(sems[s], v)
                waited[s] = v
            if op.fn is None:
                continue
            ins = op.fn(e)
            if op.semkey is not None:
                ins.then_inc(sems[op.sig[0]], 16)
            elif op.need:
                ins.then_inc(sems[op.sig[0]], 1)


class RKey(tuple):
    gen = 0
    ring = None


class Ring:
    def __init__(self, name, tensor, slot_elems, nslots):
        self.name = name
        self.t = tensor
        self.se = slot_elems
        self.n = nslots
        self.pos = 0
        self.gen = [0] * nslots

    def get(self, nelems):
        k = -(-nelems // self.se)
        if self.pos + k > self.n:
            self.pos = 0
        s0 = self.pos
        self.pos += k
        keys = []
        for sl in range(s0, s0 + k):
            self.gen[sl] += 1
            rk = RKey((self.name, sl))
            rk.gen = self.gen[sl]
            rk.ring = self
            keys.append(rk)
        return self.t[:, s0 * self.se:s0 * self.se + nelems], keys


def _check_keys(keys):
    for k in keys:
        if isinstance(k, RKey) and k.ring.gen[k[1]] != k.gen:
            raise RuntimeError(f"stale ring slot use {tuple(k)} gen {k.gen} != {k.ring.gen[k[1]]}")


def build_program(NB):
    nc = bass.Bass("TRN2", target_bir_lowering=False)
    S = Sched()

    def din(name, shape, dt=F32):
        return nc.dram_tensor(name, list(shape), dt, kind="ExternalInput").ap()

    x_d = din("x", [NB, SEQ, D])
    metap_d = din("metap", [128, D])
    w_in_d = din("w_in", [D, N_IN])
    w_oc_d = din("w_oc", [D, D])
    w_og_d = din("w_og", [D, D])
    w_o_d = din("w_o", [D, D])
    gpre_d = din("gpre_bc", [128, D])
    gpost_d = din("gpost_bc", [128, D])
    glag_d = din("glag_bc", [128, 256])
    ident_d = din("ident", [128, 128], BF16)
    tri_d = din("tri", [128, 2, 128])
    trid_d = din("trid", [128, 2, 130])
    masks_d = din("masks", [128, 512])
    wg_d = din("wg", [33, 2, 512])
    convw_d = din("convw", [128, 8, 3])
    y_d = nc.dram_tensor("y", [NB, SEQ, D], F32, kind="ExternalOutput").ap()

    es = ExitStack()

    def sb(name, shape, dt):
        return es.enter_context(nc.sbuf_tensor("sb_" + name, list(shape), dt))

    uT = sb("uT", [128, 8, SEQ], BF16)
    onT = sb("onT", [128, 8, SEQ], BF16)
    G = sb("G", [128, 24576], BF16)
    lrT = sb("lrT", [33, SEQ], BF16)
    wbuf = sb("wbuf", [128, 4, 8, 256], BF16)
    junk = sb("junk", [128, D], BF16)
    NSF, NSB, NSQ, NSP = 9, 16, 7, 3
    scrF_t = sb("scrF", [128, NSF * 512], F32)
    scrB_t = sb("scrB", [128, NSB * 512], BF16)
    scrQ_t = sb("scrQ", [128, NSQ * 512], BF16)
    scrP_t = sb("scrP", [128, NSP * 512], F32R)
    stF = sb("stF", [128, 2, 256], F32)
    stB = sb("stB", [128, 2, 256], F32)
    SbB = sb("SbB", [128, 2, 2, 256], BF16)
    stat = sb("stat", [128, 128], F32)
    gpre = sb("gpre", [128, D], F32)
    gpost = sb("gpost", [128, D], F32)
    glag = sb("glag", [128, 256], F32)
    ident = sb("ident", [128, 128], BF16)
    tri = sb("tri", [128, 2, 128], F32R)
    trid = sb("trid", [128, 2, 130], F32R)
    masks = sb("masks", [128, 512], F32)
    wg = sb("wg", [33, 2, 512], BF16)
    convw = sb("convw", [128, 8, 3], F32)
    Smeta = sb("Smeta", [128, 4, 256], F32)
    pmeta = sb("pmeta", [128, 8], BF16)
    phalo = sb("phalo", [128, 8], BF16)

    ps = [es.enter_context(nc.psum_tensor(f"ps{i}", [128, 512], F32)) for i in range(6)]
    pt = [es.enter_context(nc.psum_tensor(f"pt{i}", [128, 1024], BF16)) for i in range(2)]

    scrF = Ring("scrF", scrF_t, 512, NSF)
    scrB = Ring("scrB", scrB_t, 512, NSB)
    scrQ = Ring("scrQ", scrQ_t, 512, NSQ)
    scrP = Ring("scrP", scrP_t, 512, NSP)

    def gkeys(lo, hi):
        return [("G", g) for g in range(lo // 512, (hi - 1) // 512 + 1)]

    def qT_v(hl, t0, n):
        o = hl * 2048 + t0
        return G[:, o:o + n], gkeys(o, o + n)

    def kT_v(hl, t0, n):
        o = 4096 + hl * 2048 + t0
        return G[:, o:o + n], gkeys(o, o + n)

    def qk_tile(base, n):
        v = G[:, base:base + 4096].rearrange("p (h t) -> p h t", h=2)[:, :, n * 128:(n + 1) * 128]
        ks = gkeys(base + n * 128, base + n * 128 + 128) + gkeys(base + 2048 + n * 128, base + 2048 + n * 128 + 128)
        return v, ks

    def v_v(i, c0, n):
        o = 8192 + i * 512 + c0
        return G[:, o:o + n], gkeys(o, o + n)

    def Sf_v(n, hl):
        o = 16384 + n * 512 + hl * 256
        return G[:, o:o + 256], gkeys(o, o + 256)

    def Sf_tile(n):
        o = 16384 + n * 512
        return G[:, o:o + 512], gkeys(o, o + 512)

    def ycT_v(j, tl0, n):
        o = j * 1024 + tl0
        return G[:, o:o + n], gkeys(o, o + n)

    def mT_v(d, tl0, n):
        o = 8192 + d * 1024 + tl0
        return G[:, o:o + n], gkeys(o, o + n)

    WO0 = 16384
    wo_view = G[:, WO0:WO0 + 8192].rearrange("p (k n) -> p k n", k=8)
    wo_keys = gkeys(WO0, WO0 + 8192)

    def uT_keys(t0, n):
        return [("uT", i) for i in range(t0 // 128, (t0 + n - 1) // 128 + 1)]

    cnt = {"ps": 0, "pt": 0, "st": 0, "w": 0}

    def nps():
        i = cnt["ps"] % 6
        cnt["ps"] += 1
        return ps[i], [("ps", i)]

    def npt():
        i = cnt["pt"] % 2
        cnt["pt"] += 1
        return pt[i], [("pt", i)]

    def nstat(n=1):
        i = cnt["st"] % 32
        cnt["st"] += 1
        return stat[:, i * 4:i * 4 + n], [("st", i)]

    def mm(out, lhsT, rhs, start, stop, r, w):
        S.add("pe", lambda e: e.matmul(out, lhsT=lhsT, rhs=rhs, start=start, stop=stop), r, w)

    def tr(out, in_, r, w):
        S.add("pe", lambda e: e.transpose(out, in_, ident[:]), r + ["ident"], w)

    def act(out, in_, func, r, w, bias=None, scale=None, accum=None):
        kw = {}
        if bias is not None:
            kw["bias"] = bias
        if scale is not None:
            kw["scale"] = scale
        if accum is not None:
            kw["accum_out"] = accum
        S.add("act", lambda e: e.activation(out=out, in_=in_, func=func, **kw), r, w)

    def tt(eng, out, a, b, op, r, w):
        S.add(eng, lambda e: e.tensor_tensor(out=out, in0=a, in1=b, op=op), r, w)

    def stt(out, in0, scalar, in1, op0, op1, r, w):
        S.add("dve", lambda e: e.scalar_tensor_tensor(out=out, in0=in0, scalar=scalar, in1=in1, op0=op0, op1=op1), r, w)

    def cp(eng, out, in_, r, w):
        if eng == "act":
            S.add(eng, lambda e: e.activation(out=out, in_=in_, func=AF.Copy), r, w)
        else:
            S.add(eng, lambda e: e.tensor_copy(out=out, in_=in_), r, w)

    def mset(eng, ap, val, w):
        S.add(eng, lambda e: e.memset(ap, val), [], w)

    def dma(q, out, in_, r, w, semkey):
        return S.add(q, lambda e: e.dma_start(out=out, in_=in_), r, w, semkey=semkey)

    out_dmas = []

    S.shared_all.add("c")
    for (dst, src, key, q) in [
        (gpre[:], gpre_d, "gpre", "sp"), (gpost[:], gpost_d, "gpost", "sp"), (glag[:], glag_d, "glag", "sp"),
        (ident[:], ident_d, "ident", "sp"),
        (masks[:], masks_d, "masks", "sp"), (convw[:], convw_d, "convw", "sp"),
    ]:
        dma(q, dst, src, [], [key], "c")
    t1_, t1k_ = scrF.get(256)
    dma("sp", t1_.rearrange("p (a b) -> p a b", a=2), tri_d, [], t1k_, "c")
    cp("dve", tri[:], t1_.rearrange("p (a b) -> p a b", a=2), t1k_, ["tri"])
    t2_, t2k_ = scrF.get(260)
    dma("sp", t2_.rearrange("p (a b) -> p a b", a=2), trid_d, [], t2k_, "c")
    cp("dve", trid[:], t2_.rearrange("p (a b) -> p a b", a=2), t2k_, ["trid"])
    dma("pool", wg[:], wg_d, [], ["wg"], "cw")
    mset("pool", lrT[32:33, :], 1.0, ["lr1"])

    items = []

    def wload(spec):
        slot = cnt["w"] % 4
        cnt["w"] += 1
        ncols = spec.shape[1]
        src = spec.rearrange("(ko ki) n -> ki ko n", ki=128)
        dma("pool", wbuf[:, slot, :, 0:ncols], src, [], [("w", slot)], ("w", slot))
        return slot

    def run_items():
        blocks = []
        for i, (spec, fn) in enumerate(items):
            if spec is None:
                continue
            for pos, sp_ in enumerate(spec if isinstance(spec, list) else [spec]):
                blocks.append([i, pos, sp_])
        slots = {}
        issued = 0
        consumed = 0
        for i, (spec, fn) in enumerate(items):
            nmine = 0 if spec is None else (len(spec) if isinstance(spec, list) else 1)
            while issued < len(blocks) and (issued - consumed < 4 or blocks[issued][0] <= i):
                j, pos, sp_ = blocks[issued]
                if pos == 0 and isinstance(items[j][0], list) and len(items[j][0]) == 3 and cnt["w"] % 4 == 2:
                    b0, b1, b2 = blocks[issued], blocks[issued + 1], blocks[issued + 2]
                    blocks[issued], blocks[issued + 1], blocks[issued + 2] = b1, b2, b0
                    j, pos, sp_ = blocks[issued]
                slots.setdefault(j, {})[pos] = wload(sp_)
                issued += 1
            if spec is None:
                fn(None)
            elif isinstance(spec, list):
                fn([slots[i][p] for p in range(len(spec))])
            else:
                fn(slots[i][0])
            consumed += nmine
        items.clear()

    def proj_fm(slot, coff, M, t0, ntok, bank, bkeys):
        for k in range(8):
            mm(bank[0:M, 0:ntok], wbuf[:, slot, k, coff:coff + M], uT[:, k, t0:t0 + ntok],
               k == 0, k == 7, [("w", slot)] + uT_keys(t0, ntok), bkeys)

    def stage0(src_tile_ap_fn, ntiles):
        cx = {}

        def A(i):
            j = i % 8
            xv = onT[:, j, :].bitcast(F32)
            xk = [("on", j, s4) for s4 in range(4)]
            dma("sp", xv, src_tile_ap_fn(i), [], xk, ("xt", "on", j))
            cx[i] = {"x": (xv, xk)}

        def B(i):
            xv, xk = cx[i]["x"]
            ms, msk = nstat()
            act(junk[:], xv, AF.Square, xk, msk, scale=1.0 / 32.0, accum=ms)
            ln, lnk = nstat()
            act(ln, ms, AF.Ln, msk, lnk, bias=EPS)
            rs, rsk = nstat()
            act(rs, ln, AF.Exp, lnk, rsk, scale=-0.5)
            cx[i]["rs"] = (rs, rsk)

        def C(i):
            xv, xk = cx[i]["x"]
            rs, rsk = cx[i]["rs"]
            us, usk = scrB.get(1024)
            stt(us, xv, rs, gpre[:], ALU.mult, ALU.mult, xk + ["gpre"] + rsk, usk)
            cx[i]["us"] = (us, usk)

        def Dd(i):
            us, usk = cx[i]["us"]
            tb, tbk = npt()
            for k in range(8):
                tr(tb[:, k * 128:(k + 1) * 128], us[:, k * 128:(k + 1) * 128], usk, tbk)
            cp("dve", uT[:, :, i * 128:(i + 1) * 128], tb[:, 0:1024].rearrange("p (k t) -> p k t", k=8),
               tbk, [("uT", i)])
            del cx[i]

        phases = [(5, C), (6, Dd), (0, A), (4, B)]
        for it in range(ntiles + 6):
            for lag, ph in phases:
                i = it - lag
                if 0 <= i < ntiles:
                    ph(i)

    def stage1(hp, ntiles, need_q, need_lr):
        T = ntiles * 128
        sts = [(s * 512, min(512, T - s * 512)) for s in range((T + 511) // 512)]

        def f_qk(base_view, scale):
            def fn(slot):
                for hl in range(2):
                    for (t0, n) in sts:
                        bank, bk = nps()
                        proj_fm(slot, hl * 128, 128, t0, n, bank, bk)
                        dst, dk = base_view(hl, t0, n)
                        act(dst, bank[:, 0:n], AF.Copy, bk, dk, scale=scale)
            return fn

        items.append((w_in_d[:, C_K + hp * 256:C_K + hp * 256 + 256], f_qk(kT_v, 1.0)))

        def f_v(blk):
            def fn(slot):
                for i in range(ntiles):
                    bank, bk = nps()
                    for k in range(8):
                        mm(bank[:, 0:256], uT[:, k, i * 128:(i + 1) * 128], wbuf[:, slot, k, 0:256],
                           k == 0, k == 7, [("w", slot), ("uT", i)], bk)
                    dst, dk = v_v(i, blk * 256, 256)
                    cp("dve", dst, bank[:, 0:256], bk, dk)
            return fn

        if ntiles == 1:
            for blk in range(2):
                c0 = C_V + hp * 512 + blk * 256
                items.append((w_in_d[:, c0:c0 + 256], f_v(blk)))

        if need_lr:
            def f_lr(slot):
                for (t0, n) in sts:
                    bank, bk = nps()
                    proj_fm(slot, 0, 32, t0, n, bank, bk)
                    act(lrT[0:32, t0:t0 + n], bank[0:32, 0:n], AF.Copy, bk, [("lr", t0 // 512)])
            items.append((w_in_d[:, C_LR:C_LR + 32], f_lr))

    def gate_sp(n, hp, dirs):
        nd = len(dirs)
        bank, bk = nps()
        for di, d in enumerate(dirs):
            mm(bank[:, di * 256:(di + 1) * 256], lrT[0:33, n * 128:(n + 1) * 128], wg[0:33, d, hp * 256:(hp + 1) * 256],
               True, True, [("lr", n // 4), "lr1", "wg"], bk)
        e1, e1k = scrF.get(512)
        act(e1[:, 0:nd * 256], bank[:, 0:nd * 256], AF.Exp, bk, e1k, scale=-1.0)
        sp, spk = scrP.get(512)
        act(sp[:, 0:nd * 256], e1[:, 0:nd * 256], AF.Ln, e1k, spk, bias=1.0)
        return sp, spk

    def Q2a(c, n, spoff, d):
        sp, spk = c["sp"]
        bank, bk = nps()
        for hl in range(2):
            mm(bank[:, hl * 130:(hl + 1) * 130], sp[:, spoff + hl * 128:spoff + (hl + 1) * 128], trid[:, d, :],
               True, True, spk + ["trid"], bk)
        b3 = bank[:, 0:260].rearrange("p (h t) -> p h t", h=2)
        ed, edk = scrB.get(256)
        ed3 = ed.rearrange("p (h t) -> p h t", h=2)
        act(ed3, b3[:, :, 0:128], AF.Exp, bk, edk)
        dec, deck = nstat(2)
        act(dec.rearrange("p (h o) -> p h o", o=1), b3[:, :, 128:129], AF.Exp, bk, deck)
        c["dec"] = (dec, deck)
        c["ed"] = (ed3, edk)

    def Q2b(c, n):
        ed3, edk = c["ed"]
        kt, ktk = qk_tile(4096, n)
        kdT, kdTk = scrB.get(256)
        tt("dve", kdT.rearrange("p (h t) -> p h t", h=2), kt, ed3, ALU.mult, ktk + edk, kdTk)
        c["kdT"] = (kdT, kdTk)

    def Q3(c, n):
        kdT, kdTk = c["kdT"]
        tb, tbk = npt()
        for hl in range(2):
            tr(tb[:, hl * 128:(hl + 1) * 128], kdT[:, hl * 128:(hl + 1) * 128], kdTk, tbk)
        kd, kdk = scrB.get(256)
        cp("dve", kd, tb[:, 0:256], tbk, kdk)
        c["kd"] = (kd, kdk)

    def Q4(c, n, st, stk):
        kd, kdk = c["kd"]
        dec, deck = c["dec"]
        bank2, bk2 = nps()
        for hl in range(2):
            vv, vk = v_v(n, hl * 256, 256)
            mm(bank2[:, hl * 256:(hl + 1) * 256], kd[:, hl * 128:(hl + 1) * 128], vv, True, True, kdk + vk, bk2)
        for hl in range(2):
            stt(st[:, hl, :], st[:, hl, :], dec[:, hl:hl + 1], bank2[:, hl * 256:(hl + 1) * 256],
                ALU.mult, ALU.add, stk + deck + bk2, stk)

    def pipeline_gen(order, phases, filler=None):
        ph = [(p if isinstance(p, tuple) else (i, p)) for i, p in enumerate(phases)]
        nt = len(order)
        maxlag = max(l for l, _ in ph)
        for it in range(nt + maxlag):
            for lag, fn in ph:
                i = it - lag
                if 0 <= i < nt:
                    fn(order[i])
            if filler is not None:
                filler(it)
            yield it
        if filler is not None:
            filler(None)

    fastv = []

    def passF_gen(hp, ntiles, meta, wslots):
        if True:
            qgroups = []
            vtiles = []
            if not meta:
                qslot, vslot0, vslot1 = wslots
                fastv.append(vslot1 == vslot0 + 1)
                vtiles = list(range(ntiles))

            def emit_v(i):
                if vslot1 == vslot0 + 1:
                    bank, bk = nps()
                    for k in range(8):
                        mm(bank[:, 0:512].rearrange("p (a b) -> p a b", a=2), uT[:, k, i * 128:(i + 1) * 128],
                           wbuf[:, vslot0:vslot0 + 2, k, 0:256], k == 0, k == 7,
                           [("w", vslot0), ("w", vslot1), ("uT", i)], bk)
                    dst, dk = v_v(i, 0, 512)
                    cp("dve", dst, bank[:, 0:512], bk, dk)
                    return
                for blk, vs in ((0, vslot0), (1, vslot1)):
                    bank, bk = nps()
                    for k in range(8):
                        mm(bank[:, 0:256], uT[:, k, i * 128:(i + 1) * 128], wbuf[:, vs, k, 0:256],
                           k == 0, k == 7, [("w", vs), ("uT", i)], bk)
                    dst, dk = v_v(i, blk * 256, 256)
                    cp("dve", dst, bank[:, 0:256], bk, dk)

            def emit_q(g):
                hl, t0, n = g
                bank, bk = nps()
                proj_fm(qslot, hl * 128, 128, t0, n, bank, bk)
                dst, dk = qT_v(hl, t0, n)
                act(dst, bank[:, 0:n], AF.Copy, bk, dk, scale=128.0 ** -0.5)

            def filler(it):
                if it is None:
                    while vtiles:
                        emit_v(vtiles.pop(0))
                    while qgroups:
                        emit_q(qgroups.pop(0))
                    return
                if vtiles:
                    emit_v(vtiles.pop(0))
                if it % 2 == 1 and qgroups:
                    emit_q(qgroups.pop(0))

            if not meta:
                emit_v(vtiles.pop(0))
                emit_v(vtiles.pop(0))

            if meta:
                mset("pool", stF[:], 0.0, ["stF"])
            else:
                cp("pool", stF[:], Smeta[:, hp * 2:(hp + 1) * 2, :], ["Smeta"], ["stF"])
                d0, d0k = Sf_tile(0)
                cp("pool", d0, Smeta[:, hp * 2:(hp + 1) * 2, :].rearrange("p h e -> p (h e)"), ["Smeta"], d0k)
            last = ntiles if meta else ntiles - 1
            cx = {}

            def F1(n):
                cx[n] = {"sp": gate_sp(n, hp, [0])}

            def F2(n):
                Q2a(cx[n], n, 0, 0)

            def F2b(n):
                Q2b(cx[n], n)

            def F3(n):
                Q3(cx[n], n)

            def F4(n):
                Q4(cx[n], n, stF, ["stF"])
                if not meta:
                    dn, dnk = Sf_tile(n + 1)
                    cp("act", dn, stF[:].rearrange("p h e -> p (h e)"), ["stF"], dnk)
                del cx[n]

            yield from pipeline_gen(list(range(last)), [F1, F2, F2b, F3, F4], filler)
            if meta:
                cp("pool", Smeta[:, hp * 2:(hp + 1) * 2, :], stF[:], ["stF"], ["Smeta"])

    def passF_meta(hp):
        def fn(_):
            for _it in passF_gen(hp, 1, True, None):
                pass
        items.append((None, fn))

    def gla_pair(hp, ntiles, overlap=3):
        def fn(wslots):
            gF = passF_gen(hp, ntiles, False, wslots)
            gB = passB_gen(hp, ntiles, wslots[0])
            nF = (ntiles - 1) + 4
            for _ in range(nF - overlap):
                next(gF)
            for _ in range(overlap):
                next(gF)
                next(gB)
            for _it in gF:
                pass
            for _it in gB:
                pass
        cv = C_V + hp * 512
        items.append(([w_in_d[:, C_Q + hp * 256:C_Q + hp * 256 + 256], w_in_d[:, cv:cv + 256],
                       w_in_d[:, cv + 256:cv + 512]], fn))

    def passB_gen(hp, ntiles, qslot):
        if True:
            qsched = {0: (0, 3), 1: (1, 3), 3: (0, 2), 4: (1, 2), 7: (0, 1), 8: (1, 1), 11: (0, 0), 12: (1, 0)}

            def qfill(it):
                if it in qsched:
                    hl, s4 = qsched[it]
                    bank, bk = nps()
                    proj_fm(qslot, hl * 128, 128, s4 * 512, 512, bank, bk)
                    dst, dk = qT_v(hl, s4 * 512, 512)
                    act(dst, bank[:, 0:512], AF.Copy, bk, dk, scale=128.0 ** -0.5)

            mset("pool", stB[:], 0.0, ["stB"])
            mset("pool", SbB[:, 0], 0.0, [("SbB", 0)])
            cur = [0]
            cx = {}

            def P1(n):
                cx[n] = {"sp": gate_sp(n, hp, [0, 1])}

            def P2a(n):
                c = cx[n]
                sp, spk = c["sp"]
                bb, bbk = nps()
                for d in range(2):
                    for hl in range(2):
                        cc = (d * 2 + hl) * 128
                        mm(bb[:, cc:cc + 128], sp[:, cc:cc + 128], tri[:, d, :], True, True, spk + ["tri"], bbk)
                E, Ek = scrB.get(512)
                act(E, bb[:, 0:512], AF.Exp, bbk, Ek)
                Ei, Eik = scrB.get(512)
                act(Ei, bb[:, 0:512], AF.Exp, bbk, Eik, scale=-1.0)
                c["E"] = (E, Ek)
                c["Ei"] = (Ei, Eik)
                if n > 0:
                    Q2a(c, n, 256, 1)

            def P2b(n):
                c = cx[n]
                E, Ek = c["E"]
                Ei, Eik = c["Ei"]
                qt, qtk = qk_tile(0, n)
                kt, ktk = qk_tile(4096, n)
                qin, qink = scrQ.get(512)
                kin, kink = scrQ.get(512)
                for d in range(2):
                    tt("dve", qin[:, d * 256:(d + 1) * 256].rearrange("p (h t) -> p h t", h=2), qt,
                       E[:, d * 256:(d + 1) * 256].rearrange("p (h t) -> p h t", h=2), ALU.mult, qtk + Ek, qink)
                    tt("dve", kin[:, d * 256:(d + 1) * 256].rearrange("p (h t) -> p h t", h=2), kt,
                       Ei[:, d * 256:(d + 1) * 256].rearrange("p (h t) -> p h t", h=2), ALU.mult, ktk + Eik, kink)
                c["qin"] = (qin, qink)
                c["kin"] = (kin, kink)
                if n > 0:
                    Q2b(c, n)

            def P3(n):
                c = cx[n]
                qin, qink = c["qin"]
                kin, kink = c["kin"]
                bS, bSk = nps()
                for c4 in range(4):
                    cc = c4 * 128
                    mm(bS[:, cc:cc + 128], kin[:, cc:cc + 128], qin[:, cc:cc + 128], True, True, kink + qink, bSk)
                Sm, Smk = scrB.get(512)
                tt("dve", Sm, bS[:, 0:512], masks[:], ALU.mult, bSk + ["masks"], Smk)
                c["Sm"] = (Sm, Smk)
                if n > 0:
                    Q3(c, n)

            def P4a(n):
                c = cx[n]
                qin, qink = c["qin"]
                Sm, Smk = c["Sm"]
                bo, bok = nps()
                for hl in range(2):
                    vv, vk = v_v(n, hl * 256, 256)
                    sf, sfk = Sf_v(n, hl)
                    o_ = bo[:, hl * 256:(hl + 1) * 256]
                    mm(o_, Sm[:, hl * 128:(hl + 1) * 128], vv, True, False, Smk + vk, bok)
                    mm(o_, Sm[:, 256 + hl * 128:256 + (hl + 1) * 128], vv, False, False, Smk + vk, bok)
                    mm(o_, qin[:, hl * 128:(hl + 1) * 128], sf, False, False, qink + sfk, bok)
                    mm(o_, qin[:, 256 + hl * 128:256 + (hl + 1) * 128], SbB[:, cur[0], hl, :], False, True,
                       qink + [("SbB", cur[0])], bok)
                ms, msk = nstat(2)
                for hl in range(2):
                    act(junk[:, 0:256], bo[:, hl * 256:(hl + 1) * 256], AF.Square, bok, msk, scale=1.0 / 16.0,
                        accum=ms[:, hl:hl + 1])
                ln, lnk = nstat(2)
                act(ln, ms, AF.Ln, msk, lnk, bias=EPS)
                rs, rsk = nstat(2)
                act(rs, ln, AF.Exp, lnk, rsk, scale=-0.5)
                c["bo"] = (bo, bok)
                c["rs"] = (rs, rsk)
                if n > 0:
                    Q4(c, n, stB, ["stB"])
                    nxt = 1 - cur[0]
                    cp("pool", SbB[:, nxt].rearrange("p h e -> p (h e)"), stB[:].rearrange("p h e -> p (h e)"),
                       ["stB"], [("SbB", nxt)])
                    cur[0] = nxt

            def P4b(n):
                c = cx[n]
                bo, bok = c["bo"]
                rs, rsk = c["rs"]
                on, onk = scrB.get(512)
                for hl in range(2):
                    stt(on[:, hl * 256:(hl + 1) * 256], bo[:, hl * 256:(hl + 1) * 256], rs[:, hl:hl + 1], glag[:],
                        ALU.mult, ALU.mult, bok + rsk + ["glag"], onk)
                c["on"] = (on, onk)

            def P5(n):
                on, onk = cx[n]["on"]
                tb, tbk = npt()
                for c4 in range(4):
                    tr(tb[:, c4 * 128:(c4 + 1) * 128], on[:, c4 * 128:(c4 + 1) * 128], onk, tbk)
                cp("act", onT[:, hp * 4:(hp + 1) * 4, n * 128:(n + 1) * 128],
                   tb[:, 0:512].rearrange("p (c t) -> p c t", c=4), tbk, [("on", hp * 4 + j, n // 4) for j in range(4)])
                del cx[n]

            yield from pipeline_gen(list(range(ntiles - 1, -1, -1)),
                                    [(5, P4b), (0, P1), (1, P2a), (2, P2b), (3, P3), (4, P4a), (5, P5)], qfill)

    def stage4(hh):
        t0 = hh * 1024
        for jb in range(4):
            def fn(slot, jb=jb):
                for cl in range(2):
                    j = jb * 2 + cl
                    for sl in range(2):
                        s = hh * 2 + sl
                        bank, bk = nps()
                        proj_fm(slot, cl * 128, 128, t0 + sl * 512, 512, bank, bk)
                        sr, srk = scrB.get(512)
                        act(sr, bank[:, 0:512], AF.Silu, bk, srk)
                        o_ = onT[:, j, s * 512:(s + 1) * 512]
                        tt("pool", o_, o_, sr, ALU.mult, [("on", j, s)] + srk, [("on", j, s)])
            items.append((w_in_d[:, C_R + jb * 256:C_R + jb * 256 + 256], fn))

    def stage5(hh):
        t0 = hh * 1024
        th = t0 + 1024 if hh == 0 else t0 - 1
        hcol = 1025 if hh == 0 else 0
        for j in range(8):
            hs = {}

            def fA(slot, j=j, hs=hs):
                for sl in range(2):
                    bank, bk = nps()
                    proj_fm(slot, 0, 128, t0 + sl * 512, 512, bank, bk)
                    cc, cck = scrF.get(512)
                    act(cc, bank[:, 0:512], AF.Copy, bk, cck)
                    hs[("cc", sl)] = (cc, cck)
                if hh == 0:
                    bank, bk = nps()
                    proj_fm(slot, 0, 128, th, 1, bank, bk)
                    cch, cchk = nstat()
                    act(cch, bank[:, 0:1], AF.Copy, bk, cchk)
                    hs["cch"] = (cch, cchk)

            def fB(slot, j=j, hs=hs):
                pv, pk = scrB.get(1026)
                for sl in range(2):
                    bank, bk = nps()
                    proj_fm(slot, 0, 128, t0 + sl * 512, 512, bank, bk)
                    cc, cck = hs[("cc", sl)]
                    tt("dve", pv[:, 1 + sl * 512:1 + (sl + 1) * 512], bank[:, 0:512], cc, ALU.mult, bk + cck, pk)
                if hh == 0:
                    bank, bk = nps()
                    proj_fm(slot, 0, 128, th, 1, bank, bk)
                    cch, cchk = hs["cch"]
                    tt("dve", pv[:, hcol:hcol + 1], bank[:, 0:1], cch, ALU.mult, bk + cchk, pk)
                    cp("pool", pv[:, 0:1], pmeta[:, j:j + 1], ["pmeta"], pk)
                    cp("pool", phalo[:, j:j + 1], pv[:, 1024:1025], pk, [("phalo", j)])
                else:
                    cp("pool", pv[:, 0:1], phalo[:, j:j + 1], [("phalo", j)], pk)
                    mset("pool", pv[:, 1025:1026], 0.0, pk)
                for sl in range(2):
                    c0 = sl * 512
                    tc, tck = scrF.get(512)
                    act(tc, pv[:, 1 + c0:1 + c0 + 512], AF.Copy, pk + ["convw"], tck, scale=convw[:, j, 1:2])
                    stt(tc, pv[:, c0:c0 + 512], convw[:, j, 0:1], tc, ALU.mult, ALU.add, pk + ["convw"] + tck, tck)
                    stt(tc, pv[:, 2 + c0:2 + c0 + 512], convw[:, j, 2:3], tc, ALU.mult, ALU.add, pk + ["convw"] + tck, tck)
                    hs[("tc", sl)] = (tc, tck)

            def fC(slot, j=j, hs=hs):
                for sl in range(2):
                    bank, bk = nps()
                    proj_fm(slot, 0, 128, t0 + sl * 512, 512, bank, bk)
                    sz, szk = scrB.get(512)
                    act(sz, bank[:, 0:512], AF.Silu, bk, szk)
                    hs[("sz", sl)] = (sz, szk)

            def fD(slot, j=j, hs=hs):
                for sl in range(2):
                    bank, bk = nps()
                    proj_fm(slot, 0, 128, t0 + sl * 512, 512, bank, bk)
                    sz, szk = hs[("sz", sl)]
                    tc, tck = hs[("tc", sl)]
                    tb_, tbk_ = scrF.get(512)
                    tt("dve", tb_, bank[:, 0:512], sz, ALU.mult, bk + szk, tbk_)
                    yv, yk = ycT_v(j, sl * 512, 512)
                    tt("pool", yv, tb_, tc, ALU.mult, tbk_ + tck, yk)

            for (c0, f) in ((C_CC, fA), (C_CX, fB), (C_CZ, fC), (C_CB, fD)):
                items.append((w_in_d[:, c0 + j * 128:c0 + (j + 1) * 128], f))

    def stage6(hh):
        t0 = hh * 1024
        for d in range(8):
            st_ = {}

            def fMA(slot, st_=st_):
                for sl in range(2):
                    bank, bk = nps()
                    proj_fm(slot, 0, 128, t0 + sl * 512, 512, bank, bk)
                    sa, sak = scrF.get(512)
                    act(sa, bank[:, 0:512], AF.Sigmoid, bk, sak)
                    st_[("sa", sl)] = (sa, sak)

            def fOC(slot, d=d, st_=st_):
                for sl in range(2):
                    bank, bk = nps()
                    for c in range(8):
                        yv, yk = ycT_v(c, sl * 512, 512)
                        mm(bank[:, 0:512], wbuf[:, slot, c, 0:128], yv, c == 0, c == 7, [("w", slot)] + yk, bk)
                    sa, sak = st_[("sa", sl)]
                    t1, t1k = scrF.get(512)
                    tt("dve", t1, bank[:, 0:512], sa, ALU.mult, bk + sak, t1k)
                    st_[("t1", sl)] = (t1, t1k)

            def fMB(slot, st_=st_):
                for sl in range(2):
                    bank, bk = nps()
                    proj_fm(slot, 0, 128, t0 + sl * 512, 512, bank, bk)
                    sb_, sbk = scrF.get(512)
                    act(sb_, bank[:, 0:512], AF.Sigmoid, bk, sbk)
                    st_[("sb", sl)] = (sb_, sbk)

            def fOG(slot, d=d, st_=st_):
                for sl in range(2):
                    s = hh * 2 + sl
                    bank, bk = nps()
                    for c in range(8):
                        mm(bank[:, 0:512], wbuf[:, slot, c, 0:128], onT[:, c, s * 512:(s + 1) * 512], c == 0, c == 7,
                           [("w", slot), ("on", c, s)], bk)
                    sb_, sbk = st_[("sb", sl)]
                    t2, t2k = scrF.get(512)
                    tt("dve", t2, bank[:, 0:512], sb_, ALU.mult, bk + sbk, t2k)
                    t1, t1k = st_[("t1", sl)]
                    mv, mk = mT_v(d, sl * 512, 512)
                    tt("pool", mv, t1, t2, ALU.add, t1k + t2k, mk)

            items.append((w_in_d[:, C_MA + d * 128:C_MA + (d + 1) * 128], fMA))
            items.append((w_oc_d[:, d * 128:(d + 1) * 128], fOC))
            items.append((w_in_d[:, C_MB + d * 128:C_MB + (d + 1) * 128], fMB))
            items.append((w_og_d[:, d * 128:(d + 1) * 128], fOG))

    def stage7(b, hh):
        def fn(_):
            xs = {}

            def ldx(il):
                i = hh * 8 + il
                xv, xk = scrF.get(1024)
                dma("sp", xv, x_d[b, i * 128:(i + 1) * 128, :], [], xk, ("xt", xk[0][1]))
                xs[il] = (xv, xk)

            ldx(0)
            for il in range(8):
                i = hh * 8 + il
                if il + 1 < 8:
                    ldx(il + 1)
                banks = []
                for h2 in range(2):
                    bank, bk = nps()
                    for d in range(8):
                        mv, mk = mT_v(d, il * 128, 128)
                        mm(bank[:, 0:512], mv, wo_view[:, d, h2 * 512:(h2 + 1) * 512], d == 0, d == 7, mk + wo_keys, bk)
                    banks.append((bank, bk))
                ms, msk = nstat(2)
                for h2 in range(2):
                    act(junk[:, 0:512], banks[h2][0][:, 0:512], AF.Square, banks[h2][1], msk, scale=1.0 / 32.0,
                        accum=ms[:, h2:h2 + 1])
                m1, m1k = nstat()
                tt("dve", m1, ms[:, 0:1], ms[:, 1:2], ALU.add, msk, m1k)
                ln, lnk = nstat()
                act(ln, m1, AF.Ln, m1k, lnk, bias=EPS)
                rs, rsk = nstat()
                act(rs, ln, AF.Exp, lnk, rsk, scale=-0.5)
                yv, yk = scrF.get(1024)
                for h2 in range(2):
                    stt(yv[:, h2 * 512:(h2 + 1) * 512], banks[h2][0][:, 0:512], rs, gpost[:, h2 * 512:(h2 + 1) * 512],
                        ALU.mult, ALU.mult, banks[h2][1] + rsk + ["gpost"], yk)
                xv, xk = xs.pop(il)
                tt("pool", yv, yv, xv, ALU.add, yk + xk, yk)
                out_dmas.append(dma("sp", y_d[b, i * 128:(i + 1) * 128, :], yv, yk, [], ("y", yk[0][1])))
        items.append((None, fn))

    def load_wo():
        def fn(_):
            src = w_o_d.rearrange("(ko ki) n -> ki ko n", ki=128)
            for h2 in range(2):
                dma("pool", wo_view[:, :, h2 * 512:(h2 + 1) * 512], src[:, :, h2 * 512:(h2 + 1) * 512], [], wo_keys, "wo")
        items.append((None, fn))

    stage0(lambda i: metap_d[:, :], 1)
    for hp in range(2):
        stage1(hp, 1, False, hp == 0)
        passF_meta(hp)
    for jb in range(4):
        hs = {}

        def fA(slot, jb=jb, hs=hs):
            bank, bk = nps()
            for cl in range(2):
                for k in range(8):
                    mm(bank[:, cl * 128:(cl + 1) * 128], wbuf[:, slot, k, cl * 128:(cl + 1) * 128], uT[:, k, 0:128],
                       k == 0, k == 7, [("w", slot), ("uT", 0)], bk)
            cc, cck = scrF.get(256)
            act(cc, bank[:, 0:256], AF.Copy, bk, cck)
            hs["cc"] = (cc, cck)

        def fB(slot, jb=jb, hs=hs):
            bank, bk = nps()
            for cl in range(2):
                for k in range(8):
                    mm(bank[:, cl * 128:(cl + 1) * 128], wbuf[:, slot, k, cl * 128:(cl + 1) * 128], uT[:, k, 0:128],
                       k == 0, k == 7, [("w", slot), ("uT", 0)], bk)
            cc, cck = hs["cc"]
            for cl in range(2):
                j = jb * 2 + cl
                tt("dve", pmeta[:, j:j + 1], bank[:, cl * 128 + 127:cl * 128 + 128], cc[:, cl * 128 + 127:cl * 128 + 128],
                   ALU.mult, bk + cck, ["pmeta"])
        items.append((w_in_d[:, C_CC + jb * 256:C_CC + (jb + 1) * 256], fA))
        items.append((w_in_d[:, C_CX + jb * 256:C_CX + (jb + 1) * 256], fB))
    run_items()

    for b in range(NB):
        stage0(lambda i, b=b: x_d[b, i * 128:(i + 1) * 128, :], 16)
        for hp in range(2):
            stage1(hp, 16, True, hp == 0)
            gla_pair(hp, 16)
        load_wo()
        for hh in range(2):
            stage4(hh)
            stage5(hh)
            stage6(hh)
            stage7(b, hh)
        run_items()

    S.add("sp", None, [], [], extra=out_dmas)

    S.fastv = fastv
    S.finalize()
    sems = {}
    for e in ("pe", "act", "dve", "pool", "sp"):
        sems[("eng", e)] = es.enter_context(nc.semaphore(f"s_{e}"))
    for i, k in enumerate(S.dma_keys):
        sems[("dma", k)] = es.enter_context(nc.semaphore(f"d_{i}"))

    with nc.Block() as block:
        @block.tensor
        def _(e):
            S.emit("pe", e, sems)

        @block.scalar
        def _(e):
            S.emit("act", e, sems)

        @block.vector
        def _(e):
            S.emit("dve", e, sems)

        @block.gpsimd
        def _(e):
            S.emit("pool", e, sems)

        @block.sync
        def _(e):
            S.emit("sp", e, sems)
    es.close()
    return nc, S


def _consts():
    s = np.arange(128)[:, None]
    t = np.arange(128)[None, :]
    g = np.float32(-1.0 / 16.0)
    tri = np.zeros((128, 2, 128), np.float32)
    tri[:, 0, :] = np.where(s <= t, g, 0.0)
    tri[:, 1, :] = np.where(s >= t, g, 0.0)
    trid = np.full((128, 2, 130), g, np.float32)
    trid[:, 0, 0:128] = np.where(s > t, g, 0.0)
    trid[:, 1, 0:128] = np.where(s < t, g, 0.0)
    masks = np.zeros((128, 2, 2, 128), np.float32)
    masks[:, 0, :, :] = np.where(s <= t, 1.0, 0.0)[:, None, :]
    masks[:, 1, :, :] = np.where(s > t, 1.0, 0.0)[:, None, :]
    ident = np.eye(128, dtype=np.float32).astype(ml_dtypes.bfloat16)
    return tri, trid, masks.reshape(128, 512), ident


def make_in_maps(inputs, n_cores, NB):
    f = lambda a: np.ascontiguousarray(np.asarray(a, dtype=np.float32))
    x = f(inputs["x"])
    tri, trid, masks, ident = _consts()
    metap = np.zeros((128, D), np.float32)
    metap[112:128] = f(inputs["meta_tokens"])
    wg = np.zeros((33, 2, 512), np.float32)
    wg[0:16, 0] = f(inputs["w_gate_fwd"])[0]
    wg[32, 0] = f(inputs["b_gate_fwd"])[0]
    wg[16:32, 1] = f(inputs["w_gate_bwd"])[0]
    wg[32, 1] = f(inputs["b_gate_bwd"])[0]
    convw = np.ascontiguousarray(f(inputs["conv_w"])[0].T.reshape(8, 128, 3).transpose(1, 0, 2))
    common = {
        "metap": metap,
        "w_in": f(inputs["w_in"])[0],
        "w_oc": f(inputs["w_out_conv"])[0],
        "w_og": f(inputs["w_out_gla"])[0],
        "w_o": f(inputs["w_merge_out"])[0],
        "gpre_bc": np.ascontiguousarray(np.broadcast_to(f(inputs["norm_pre"])[0][None, :], (128, D))),
        "gpost_bc": np.ascontiguousarray(np.broadcast_to(f(inputs["norm_post"])[0][None, :], (128, D))),
        "glag_bc": np.ascontiguousarray(np.broadcast_to(f(inputs["gla_norm"])[0][None, :], (128, 256))),
        "ident": ident, "tri": tri, "trid": trid, "masks": masks, "wg": wg, "convw": convw,
    }
    maps = []
    for c in range(n_cores):
        m = dict(common)
        m["x"] = np.ascontiguousarray(x[c * NB:(c + 1) * NB])
        maps.append(m)
    return maps


_CACHE = {}


def kernel(**inputs):
    NB = inputs["x"].shape[0] // N_CORES
    if NB not in _CACHE:
        _CACHE[NB] = build_program(NB)[0]
    nc = _CACHE[NB]
    in_maps = make_in_maps(inputs, N_CORES, NB)
    res = run_bass_kernel_spmd(nc, in_maps, core_ids=list(range(N_CORES)))
    out = np.concatenate([np.asarray(r["y"]) for r in res.results], axis=0)
    return out.astype(np.float32)
```

```python
import numpy as np
import ml_dtypes
from contextlib import ExitStack

import concourse.bass as bass
import concourse.mybir as mybir
from concourse.bass_utils import run_bass_kernel_spmd

F32 = mybir.dt.float32
BF16 = mybir.dt.bfloat16
F32R = mybir.dt.float32r
AF = mybir.ActivationFunctionType
ALU = mybir.AluOpType

N_CORES = 8
D = 1024
SEQ = 2048
N_IN = 9248
EPS = 1e-6
C_CB, C_CC, C_CX, C_CZ = 0, 1024, 2048, 3072
C_Q, C_K, C_V, C_R, C_LR, C_MA, C_MB = 4096, 4608, 5120, 6144, 7168, 7200, 8224


class Op:
    __slots__ = ("eng", "fn", "deps", "sig", "need", "semkey", "idx")

    def __init__(self, eng, fn, semkey=None):
        self.eng = eng
        self.fn = fn
        self.deps = set()
        self.sig = None
        self.need = False
        self.semkey = semkey
        self.idx = None


class Sched:
    ENGS = ("pe", "act", "dve", "pool", "sp")

    def __init__(self):
        self.ops = {e: [] for e in self.ENGS}
        self.last_w = {}
        self.readers = {}
        self.shared_all = set()

    def add(self, eng, fn, reads=(), writes=(), semkey=None, extra=()):
        op = Op(eng, fn, semkey)
        _check_keys(reads)
        _check_keys(writes)
        deps = op.deps
        deps.update(extra)
        for k in reads:
            w = self.last_w.get(k)
            if w is not None:
                deps.add(w)
        for k in writes:
            w = self.last_w.get(k)
            if w is not None:
                deps.add(w)
            for r in self.readers.get(k, ()):
                deps.add(r)
        deps.discard(op)
        if eng == "pe":
            op.deps = {d for d in deps if not (d.eng == "pe" and d.semkey is None)}
        for k in reads:
            self.readers.setdefault(k, []).append(op)
        for k in writes:
            self.last_w[k] = op
            self.readers[k] = []
        op.idx = len(self.ops[eng])
        self.ops[eng].append(op)
        return op

    def finalize(self):
        for e in self.ENGS:
            for op in self.ops[e]:
                best = {}
                keep = set()
                for d in op.deps:
                    if d.semkey is not None:
                        keep.add(d)
                    elif d.eng not in best or d.idx > best[d.eng].idx:
                        best[d.eng] = d
                keep.update(best.values())
                op.deps = keep
                for d in op.deps:
                    d.need = True
        self.nsig = {}
        for e in self.ENGS:
            c = 0
            for op in self.ops[e]:
                if op.semkey is None and op.need:
                    c += 1
                    op.sig = (("eng", e), c)
        cnt = {}
        dma_ops = []
        for e in self.ENGS:
            for op in self.ops[e]:
                if op.semkey is not None:
                    cnt[op.semkey] = cnt.get(op.semkey, 0) + 1
                    op.sig = (("dma", op.semkey), 16 * cnt[op.semkey])
                    dma_ops.append(op)
        for op in dma_ops:
            if op.semkey in self.shared_all:
                op.sig = (("dma", op.semkey), 16 * cnt[op.semkey])
        self.dma_keys = list(cnt.keys())

    def emit(self, eng, e, sems):
        waited = {}
        for op in self.ops[eng]:
            need = {}
            for d in op.deps:
                s, v = d.sig
                if v > need.get(s, 0):
                    need[s] = v
            for s, v in need.items():
                if waited.get(s, 0) >= v:
                    continue
                e.wait_ge(sems[s], v)
                waited[s] = v
            if op.fn is None:
                continue
            ins = op.fn(e)
            if op.semkey is not None:
                ins.then_inc(sems[op.sig[0]], 16)
            elif op.need:
                ins.then_inc(sems[op.sig[0]], 1)


class RKey(tuple):
    gen = 0
    ring = None


class Ring:
    def __init__(self, name, tensor, slot_elems, nslots):
        self.name = name
        self.t = tensor
        self.se = slot_elems
        self.n = nslots
        self.pos = 0
        self.gen = [0] * nslots

    def get(self, nelems):
        k = -(-nelems // self.se)
        if self.pos + k > self.n:
            self.pos = 0
        s0 = self.pos
        self.pos += k
        keys = []
        for sl in range(s0, s0 + k):
            self.gen[sl] += 1
            rk = RKey((self.name, sl))
            rk.gen = self.gen[sl]
            rk.ring = self
            keys.append(rk)
        return self.t[:, s0 * self.se:s0 * self.se + nelems], keys


def _check_keys(keys):
    for k in keys:
        if isinstance(k, RKey) and k.ring.gen[k[1]] != k.gen:
            raise RuntimeError(f"stale ring slot use {tuple(k)} gen {k.gen} != {k.ring.gen[k[1]]}")


def build_program(NB):
    nc = bass.Bass("TRN2", target_bir_lowering=False)
    S = Sched()

    def din(name, shape, dt=F32):
        return nc.dram_tensor(name, list(shape), dt, kind="ExternalInput").ap()

    x_d = din("x", [NB, SEQ, D])
    metap_d = din("metap", [128, D])
    w_in_d = din("w_in", [D, N_IN])
    w_oc_d = din("w_oc", [D, D])
    w_og_d = din("w_og", [D, D])
    w_o_d = din("w_o", [D, D])
    gpre_d = din("gpre_bc", [128, D])
    gpost_d = din("gpost_bc", [128, D])
    glag_d = din("glag_bc", [128, 256])
    ident_d = din("ident", [128, 128], BF16)
    tri_d = din("tri", [128, 2, 128])
    trid_d = din("trid", [128, 2, 130])
    masks_d = din("masks", [128, 512])
    wg_d = din("wg", [33, 2, 512])
    convw_d = din("convw", [128, 8, 3])
    y_d = nc.dram_tensor("y", [NB, SEQ, D], F32, kind="ExternalOutput").ap()

    es = ExitStack()

    def sb(name, shape, dt):
        return es.enter_context(nc.sbuf_tensor("sb_" + name, list(shape), dt))

    uT = sb("uT", [128, 8, SEQ], BF16)
    onT = sb("onT", [128, 8, SEQ], BF16)
    G = sb("G", [128, 24576], BF16)
    lrT = sb("lrT", [33, SEQ], BF16)
    wbuf = sb("wbuf", [128, 4, 8, 256], BF16)
    junk = sb("junk", [128, D], BF16)
    NSF, NSB, NSQ, NSP = 9, 16, 7, 3
    scrF_t = sb("scrF", [128, NSF * 512], F32)
    scrB_t = sb("scrB", [128, NSB * 512], BF16)
    scrQ_t = sb("scrQ", [128, NSQ * 512], BF16)
    scrP_t = sb("scrP", [128, NSP * 512], F32R)
    stF = sb("stF", [128, 2, 256], F32)
    stB = sb("stB", [128, 2, 256], F32)
    SbB = sb("SbB", [128, 2, 2, 256], BF16)
    stat = sb("stat", [128, 128], F32)
    gpre = sb("gpre", [128, D], F32)
    gpost = sb("gpost", [128, D], F32)
    glag = sb("glag", [128, 256], F32)
    ident = sb("ident", [128, 128], BF16)
    tri = sb("tri", [128, 2, 128], F32R)
    trid = sb("trid", [128, 2, 130], F32R)
    masks = sb("masks", [128, 512], F32)
    wg = sb("wg", [33, 2, 512], BF16)
    convw = sb("convw", [128, 8, 3], F32)
    Smeta = sb("Smeta", [128, 4, 256], F32)
    pmeta = sb("pmeta", [128, 8], BF16)
    phalo = sb("phalo", [128, 8], BF16)

    ps = [es.enter_context(nc.psum_tensor(f"ps{i}", [128, 512], F32)) for i in range(6)]
    pt = [es.enter_context(nc.psum_tensor(f"pt{i}", [128, 1024], BF16)) for i in range(2)]

    scrF = Ring("scrF", scrF_t, 512, NSF)
    scrB = Ring("scrB", scrB_t, 512, NSB)
    scrQ = Ring("scrQ", scrQ_t, 512, NSQ)
    scrP = Ring("scrP", scrP_t, 512, NSP)

    def gkeys(lo, hi):
        return [("G", g) for g in range(lo // 512, (hi - 1) // 512 + 1)]

    def qT_v(hl, t0, n):
        o = hl * 2048 + t0
        return G[:, o:o + n], gkeys(o, o + n)

    def kT_v(hl, t0, n):
        o = 4096 + hl * 2048 + t0
        return G[:, o:o + n], gkeys(o, o + n)

    def qk_tile(base, n):
        v = G[:, base:base + 4096].rearrange("p (h t) -> p h t", h=2)[:, :, n * 128:(n + 1) * 128]
        ks = gkeys(base + n * 128, base + n * 128 + 128) + gkeys(base + 2048 + n * 128, base + 2048 + n * 128 + 128)
        return v, ks

    def v_v(i, c0, n):
        o = 8192 + i * 512 + c0
        return G[:, o:o + n], gkeys(o, o + n)

    def Sf_v(n, hl):
        o = 16384 + n * 512 + hl * 256
        return G[:, o:o + 256], gkeys(o, o + 256)

    def Sf_tile(n):
        o = 16384 + n * 512
        return G[:, o:o + 512], gkeys(o, o + 512)

    def ycT_v(j, tl0, n):
        o = j * 1024 + tl0
        return G[:, o:o + n], gkeys(o, o + n)

    def mT_v(d, tl0, n):
        o = 8192 + d * 1024 + tl0
        return G[:, o:o + n], gkeys(o, o + n)

    WO0 = 16384
    wo_view = G[:, WO0:WO0 + 8192].rearrange("p (k n) -> p k n", k=8)
    wo_keys = gkeys(WO0, WO0 + 8192)

    def uT_keys(t0, n):
        return [("uT", i) for i in range(t0 // 128, (t0 + n - 1) // 128 + 1)]

    cnt = {"ps": 0, "pt": 0, "st": 0, "w": 0}

    def nps():
        i = cnt["ps"] % 6
        cnt["ps"] += 1
        return ps[i], [("ps", i)]

    def npt():
        i = cnt["pt"] % 2
        cnt["pt"] += 1
        return pt[i], [("pt", i)]

    def nstat(n=1):
        i = cnt["st"] % 32
        cnt["st"] += 1
        return stat[:, i * 4:i * 4 + n], [("st", i)]

    def mm(out, lhsT, rhs, start, stop, r, w):
        S.add("pe", lambda e: e.matmul(out, lhsT=lhsT, rhs=rhs, start=start, stop=stop), r, w)

    def tr(out, in_, r, w):
        S.add("pe", lambda e: e.transpose(out, in_, ident[:]), r + ["ident"], w)

    def act(out, in_, func, r, w, bias=None, scale=None, accum=None):
        kw = {}
        if bias is not None:
            kw["bias"] = bias
        if scale is not None:
            kw["scale"] = scale
        if accum is not None:
            kw["accum_out"] = accum
        S.add("act", lambda e: e.activation(out=out, in_=in_, func=func, **kw), r, w)

    def tt(eng, out, a, b, op, r, w):
        S.add(eng, lambda e: e.tensor_tensor(out=out, in0=a, in1=b, op=op), r, w)

    def stt(out, in0, scalar, in1, op0, op1, r, w):
        S.add("dve", lambda e: e.scalar_tensor_tensor(out=out, in0=in0, scalar=scalar, in1=in1, op0=op0, op1=op1), r, w)

    def cp(eng, out, in_, r, w):
        if eng == "act":
            S.add(eng, lambda e: e.activation(out=out, in_=in_, func=AF.Copy), r, w)
        else:
            S.add(eng, lambda e: e.tensor_copy(out=out, in_=in_), r, w)

    def mset(eng, ap, val, w):
        S.add(eng, lambda e: e.memset(ap, val), [], w)

    def dma(q, out, in_, r, w, semkey):
        return S.add(q, lambda e: e.dma_start(out=out, in_=in_), r, w, semkey=semkey)

    out_dmas = []

    S.shared_all.add("c")
    for (dst, src, key, q) in [
        (gpre[:], gpre_d, "gpre", "sp"), (gpost[:], gpost_d, "gpost", "sp"), (glag[:], glag_d, "glag", "sp"),
        (ident[:], ident_d, "ident", "sp"),
        (masks[:], masks_d, "masks", "sp"), (convw[:], convw_d, "convw", "sp"),
    ]:
        dma(q, dst, src, [], [key], "c")
    t1_, t1k_ = scrF.get(256)
    dma("sp", t1_.rearrange("p (a b) -> p a b", a=2), tri_d, [], t1k_, "c")
    cp("dve", tri[:], t1_.rearrange("p (a b) -> p a b", a=2), t1k_, ["tri"])
    t2_, t2k_ = scrF.get(260)
    dma("sp", t2_.rearrange("p (a b) -> p a b", a=2), trid_d, [], t2k_, "c")
    cp("dve", trid[:], t2_.rearrange("p (a b) -> p a b", a=2), t2k_, ["trid"])
    dma("pool", wg[:], wg_d, [], ["wg"], "cw")
    mset("pool", lrT[32:33, :], 1.0, ["lr1"])

    items = []

    def wload(spec):
        slot = cnt["w"] % 4
        cnt["w"] += 1
        ncols = spec.shape[1]
        src = spec.rearrange("(ko ki) n -> ki ko n", ki=128)
        dma("pool", wbuf[:, slot, :, 0:ncols], src, [], [("w", slot)], ("w", slot))
        return slot

    def run_items():
        blocks = []
        for i, (spec, fn) in enumerate(items):
            if spec is None:
                continue
            for pos, sp_ in enumerate(spec if isinstance(spec, list) else [spec]):
                blocks.append([i, pos, sp_])
        slots = {}
        issued = 0
        consumed = 0
        for i, (spec, fn) in enumerate(items):
            nmine = 0 if spec is None else (len(spec) if isinstance(spec, list) else 1)
            while issued < len(blocks) and (issued - consumed < 4 or blocks[issued][0] <= i):
                j, pos, sp_ = blocks[issued]
                if pos == 0 and isinstance(items[j][0], list) and len(items[j][0]) == 3 and cnt["w"] % 4 == 2:
                    b0, b1, b2 = blocks[issued], blocks[issued + 1], blocks[issued + 2]
                    blocks[issued], blocks[issued + 1], blocks[issued + 2] = b1, b2, b0
                    j, pos, sp_ = blocks[issued]
                slots.setdefault(j, {})[pos] = wload(sp_)
                issued += 1
            if spec is None:
                fn(None)
            elif isinstance(spec, list):
                fn([slots[i][p] for p in range(len(spec))])
            else:
                fn(slots[i][0])
            consumed += nmine
        items.clear()

    def proj_fm(slot, coff, M, t0, ntok, bank, bkeys):
        for k in range(8):
            mm(bank[0:M, 0:ntok], wbuf[:, slot, k, coff:coff + M], uT[:, k, t0:t0 + ntok],
               k == 0, k == 7, [("w", slot)] + uT_keys(t0, ntok), bkeys)

    def stage0(src_tile_ap_fn, ntiles, pre=None):
        cx = dict(pre) if pre else {}
        done_A = set(cx.keys())

        def A(i):
            if i in done_A:
                return
            if ntiles == 1:
                xv, xk = scrF.get(1024)
                dma("sp", xv, src_tile_ap_fn(i), [], xk, ("xt", xk[0][1]))
            else:
                j = i % 8
                xv = onT[:, j, :].bitcast(F32)
                xk = [("on", j, s4) for s4 in range(4)]
                dma("sp", xv, src_tile_ap_fn(i), [], xk, ("xt", "on", j))
            cx[i] = {"x": (xv, xk)}

        def B(i):
            xv, xk = cx[i]["x"]
            ms, msk = nstat()
            act(junk[:], xv, AF.Square, xk, msk, scale=1.0 / 32.0, accum=ms)
            ln, lnk = nstat()
            act(ln, ms, AF.Ln, msk, lnk, bias=EPS)
            rs, rsk = nstat()
            act(rs, ln, AF.Exp, lnk, rsk, scale=-0.5)
            cx[i]["rs"] = (rs, rsk)

        def C(i):
            xv, xk = cx[i]["x"]
            rs, rsk = cx[i]["rs"]
            us, usk = scrB.get(1024)
            stt(us, xv, rs, gpre[:], ALU.mult, ALU.mult, xk + ["gpre"] + rsk, usk)
            cx[i]["us"] = (us, usk)

        def Dd(i):
            us, usk = cx[i]["us"]
            tb, tbk = npt()
            for k in range(8):
                tr(tb[:, k * 128:(k + 1) * 128], us[:, k * 128:(k + 1) * 128], usk, tbk)
            cp("dve", uT[:, :, i * 128:(i + 1) * 128], tb[:, 0:1024].rearrange("p (k t) -> p k t", k=8),
               tbk, [("uT", i)])
            del cx[i]

        phases = [(5, C), (6, Dd), (0, A), (4, B)]
        for it in range(ntiles + 6):
            for lag, ph in phases:
                i = it - lag
                if 0 <= i < ntiles:
                    ph(i)

    def stage0_prefetch(src_tile_ap_fn, n=8):
        pre = {}
        for i in range(n):
            j = i % 8
            xv = onT[:, j, :].bitcast(F32)
            xk = [("on", j, s4) for s4 in range(4)]
            dma("sp", xv, src_tile_ap_fn(i), [], xk, ("xt", "on", j))
            pre[i] = {"x": (xv, xk)}
        return pre

    s0pre = {}

    def stage1(hp, ntiles, need_q, need_lr):
        T = ntiles * 128
        sts = [(s * 512, min(512, T - s * 512)) for s in range((T + 511) // 512)]

        def f_qk(base_view, scale):
            def fn(slot):
                for hl in range(2):
                    for (t0, n) in sts:
                        bank, bk = nps()
                        proj_fm(slot, hl * 128, 128, t0, n, bank, bk)
                        dst, dk = base_view(hl, t0, n)
                        act(dst, bank[:, 0:n], AF.Copy, bk, dk, scale=scale)
            return fn

        items.append((w_in_d[:, C_K + hp * 256:C_K + hp * 256 + 256], f_qk(kT_v, 1.0)))

        def f_v(blk):
            def fn(slot):
                for i in range(ntiles):
                    bank, bk = nps()
                    for k in range(8):
                        mm(bank[:, 0:256], uT[:, k, i * 128:(i + 1) * 128], wbuf[:, slot, k, 0:256],
                           k == 0, k == 7, [("w", slot), ("uT", i)], bk)
                    dst, dk = v_v(i, blk * 256, 256)
                    cp("dve", dst, bank[:, 0:256], bk, dk)
            return fn

        if ntiles == 1:
            for blk in range(2):
                c0 = C_V + hp * 512 + blk * 256
                items.append((w_in_d[:, c0:c0 + 256], f_v(blk)))

        if need_lr:
            def f_lr(slot):
                for (t0, n) in sts:
                    bank, bk = nps()
                    proj_fm(slot, 0, 32, t0, n, bank, bk)
                    act(lrT[0:32, t0:t0 + n], bank[0:32, 0:n], AF.Copy, bk, [("lr", t0 // 512)])
            items.append((w_in_d[:, C_LR:C_LR + 32], f_lr))

    def gate_sp(n, hp, dirs):
        nd = len(dirs)
        bank, bk = nps()
        for di, d in enumerate(dirs):
            mm(bank[:, di * 256:(di + 1) * 256], lrT[0:33, n * 128:(n + 1) * 128], wg[0:33, d, hp * 256:(hp + 1) * 256],
               True, True, [("lr", n // 4), "lr1", "wg"], bk)
        e1, e1k = scrF.get(512)
        act(e1[:, 0:nd * 256], bank[:, 0:nd * 256], AF.Exp, bk, e1k, scale=-1.0)
        sp, spk = scrP.get(512)
        act(sp[:, 0:nd * 256], e1[:, 0:nd * 256], AF.Ln, e1k, spk, bias=1.0)
        return sp, spk

    def Q2a(c, n, spoff, d):
        sp, spk = c["sp"]
        bank, bk = nps()
        for hl in range(2):
            mm(bank[:, hl * 130:(hl + 1) * 130], sp[:, spoff + hl * 128:spoff + (hl + 1) * 128], trid[:, d, :],
               True, True, spk + ["trid"], bk)
        b3 = bank[:, 0:260].rearrange("p (h t) -> p h t", h=2)
        ed, edk = scrB.get(256)
        ed3 = ed.rearrange("p (h t) -> p h t", h=2)
        act(ed3, b3[:, :, 0:128], AF.Exp, bk, edk)
        dec, deck = nstat(2)
        act(dec.rearrange("p (h o) -> p h o", o=1), b3[:, :, 128:129], AF.Exp, bk, deck)
        c["dec"] = (dec, deck)
        c["ed"] = (ed3, edk)

    def Q2b(c, n):
        ed3, edk = c["ed"]
        kt, ktk = qk_tile(4096, n)
        kdT, kdTk = scrB.get(256)
        tt("dve", kdT.rearrange("p (h t) -> p h t", h=2), kt, ed3, ALU.mult, ktk + edk, kdTk)
        c["kdT"] = (kdT, kdTk)

    def Q3(c, n):
        kdT, kdTk = c["kdT"]
        tb, tbk = npt()
        for hl in range(2):
            tr(tb[:, hl * 128:(hl + 1) * 128], kdT[:, hl * 128:(hl + 1) * 128], kdTk, tbk)
        kd, kdk = scrB.get(256)
        cp("dve", kd, tb[:, 0:256], tbk, kdk)
        c["kd"] = (kd, kdk)

    def Q4(c, n, st, stk):
        kd, kdk = c["kd"]
        dec, deck = c["dec"]
        bank2, bk2 = nps()
        for hl in range(2):
            vv, vk = v_v(n, hl * 256, 256)
            mm(bank2[:, hl * 256:(hl + 1) * 256], kd[:, hl * 128:(hl + 1) * 128], vv, True, True, kdk + vk, bk2)
        for hl in range(2):
            stt(st[:, hl, :], st[:, hl, :], dec[:, hl:hl + 1], bank2[:, hl * 256:(hl + 1) * 256],
                ALU.mult, ALU.add, stk + deck + bk2, stk)

    def pipeline_gen(order, phases, filler=None):
        ph = [(p if isinstance(p, tuple) else (i, p)) for i, p in enumerate(phases)]
        nt = len(order)
        maxlag = max(l for l, _ in ph)
        for it in range(nt + maxlag):
            for lag, fn in ph:
                i = it - lag
                if 0 <= i < nt:
                    fn(order[i])
            if filler is not None:
                filler(it)
            yield it
        if filler is not None:
            filler(None)

    fastv = []

    def passF_gen(hp, ntiles, meta, wslots):
        if True:
            qgroups = []
            vtiles = []
            if not meta:
                qslot, vslot0, vslot1 = wslots
                fastv.append(vslot1 == vslot0 + 1)
                vtiles = list(range(ntiles))

            def emit_v(i):
                if vslot1 == vslot0 + 1:
                    bank, bk = nps()
                    for k in range(8):
                        mm(bank[:, 0:512].rearrange("p (a b) -> p a b", a=2), uT[:, k, i * 128:(i + 1) * 128],
                           wbuf[:, vslot0:vslot0 + 2, k, 0:256], k == 0, k == 7,
                           [("w", vslot0), ("w", vslot1), ("uT", i)], bk)
                    dst, dk = v_v(i, 0, 512)
                    cp("dve", dst, bank[:, 0:512], bk, dk)
                    return
                for blk, vs in ((0, vslot0), (1, vslot1)):
                    bank, bk = nps()
                    for k in range(8):
                        mm(bank[:, 0:256], uT[:, k, i * 128:(i + 1) * 128], wbuf[:, vs, k, 0:256],
                           k == 0, k == 7, [("w", vs), ("uT", i)], bk)
                    dst, dk = v_v(i, blk * 256, 256)
                    cp("dve", dst, bank[:, 0:256], bk, dk)

            def emit_q(g):
                hl, t0, n = g
                bank, bk = nps()
                proj_fm(qslot, hl * 128, 128, t0, n, bank, bk)
                dst, dk = qT_v(hl, t0, n)
                act(dst, bank[:, 0:n], AF.Copy, bk, dk, scale=128.0 ** -0.5)

            def filler(it):
                if it is None:
                    while vtiles:
                        emit_v(vtiles.pop(0))
                    while qgroups:
                        emit_q(qgroups.pop(0))
                    return
                if vtiles:
                    emit_v(vtiles.pop(0))
                if it % 2 == 1 and qgroups:
                    emit_q(qgroups.pop(0))

            if not meta:
                emit_v(vtiles.pop(0))
                emit_v(vtiles.pop(0))

            if meta:
                mset("pool", stF[:], 0.0, ["stF"])
            else:
                cp("pool", stF[:], Smeta[:, hp * 2:(hp + 1) * 2, :], ["Smeta"], ["stF"])
                d0, d0k = Sf_tile(0)
                cp("pool", d0, Smeta[:, hp * 2:(hp + 1) * 2, :].rearrange("p h e -> p (h e)"), ["Smeta"], d0k)
            last = ntiles if meta else ntiles - 1
            cx = {}

            def F1(n):
                cx[n] = {"sp": gate_sp(n, hp, [0])}

            def F2(n):
                Q2a(cx[n], n, 0, 0)

            def F2b(n):
                Q2b(cx[n], n)

            def F3(n):
                Q3(cx[n], n)

            def F4(n):
                Q4(cx[n], n, stF, ["stF"])
                if not meta:
                    dn, dnk = Sf_tile(n + 1)
                    cp("act", dn, stF[:].rearrange("p h e -> p (h e)"), ["stF"], dnk)
                del cx[n]

            yield from pipeline_gen(list(range(last)), [F1, F2, F2b, F3, F4], filler)
            if meta:
                cp("pool", Smeta[:, hp * 2:(hp + 1) * 2, :], stF[:], ["stF"], ["Smeta"])

    def passF_meta(hp):
        def fn(_):
            for _it in passF_gen(hp, 1, True, None):
                pass
        items.append((None, fn))

    def gla_pair(hp, ntiles, overlap=3):
        def fn(wslots):
            gF = passF_gen(hp, ntiles, False, wslots)
            gB = passB_gen(hp, ntiles, wslots[0])
            nF = (ntiles - 1) + 4
            for _ in range(nF - overlap):
                next(gF)
            for _ in range(overlap):
                next(gF)
                next(gB)
            for _it in gF:
                pass
            for _it in gB:
                pass
        cv = C_V + hp * 512
        items.append(([w_in_d[:, C_Q + hp * 256:C_Q + hp * 256 + 256], w_in_d[:, cv:cv + 256],
                       w_in_d[:, cv + 256:cv + 512]], fn))

    def passB_gen(hp, ntiles, qslot):
        if True:
            qsched = {0: (0, 3), 1: (1, 3), 3: (0, 2), 4: (1, 2), 7: (0, 1), 8: (1, 1), 11: (0, 0), 12: (1, 0)}

            def qfill(it):
                if it in qsched:
                    hl, s4 = qsched[it]
                    bank, bk = nps()
                    proj_fm(qslot, hl * 128, 128, s4 * 512, 512, bank, bk)
                    dst, dk = qT_v(hl, s4 * 512, 512)
                    act(dst, bank[:, 0:512], AF.Copy, bk, dk, scale=128.0 ** -0.5)

            mset("pool", stB[:], 0.0, ["stB"])
            mset("pool", SbB[:, 0], 0.0, [("SbB", 0)])
            cur = [0]
            cx = {}

            def P1(n):
                cx[n] = {"sp": gate_sp(n, hp, [0, 1])}

            def P2a(n):
                c = cx[n]
                sp, spk = c["sp"]
                bb, bbk = nps()
                for d in range(2):
                    for hl in range(2):
                        cc = (d * 2 + hl) * 128
                        mm(bb[:, cc:cc + 128], sp[:, cc:cc + 128], tri[:, d, :], True, True, spk + ["tri"], bbk)
                E, Ek = scrB.get(512)
                act(E, bb[:, 0:512], AF.Exp, bbk, Ek)
                Ei, Eik = scrB.get(512)
                act(Ei, bb[:, 0:512], AF.Exp, bbk, Eik, scale=-1.0)
                c["E"] = (E, Ek)
                c["Ei"] = (Ei, Eik)
                if n > 0:
                    Q2a(c, n, 256, 1)

            def P2b(n):
                c = cx[n]
                E, Ek = c["E"]
                Ei, Eik = c["Ei"]
                qt, qtk = qk_tile(0, n)
                kt, ktk = qk_tile(4096, n)
                qin, qink = scrQ.get(512)
                kin, kink = scrQ.get(512)
                for d in range(2):
                    tt("dve", qin[:, d * 256:(d + 1) * 256].rearrange("p (h t) -> p h t", h=2), qt,
                       E[:, d * 256:(d + 1) * 256].rearrange("p (h t) -> p h t", h=2), ALU.mult, qtk + Ek, qink)
                    tt("dve", kin[:, d * 256:(d + 1) * 256].rearrange("p (h t) -> p h t", h=2), kt,
                       Ei[:, d * 256:(d + 1) * 256].rearrange("p (h t) -> p h t", h=2), ALU.mult, ktk + Eik, kink)
                c["qin"] = (qin, qink)
                c["kin"] = (kin, kink)
                if n > 0:
                    Q2b(c, n)

            def P3(n):
                c = cx[n]
                qin, qink = c["qin"]
                kin, kink = c["kin"]
                bS, bSk = nps()
                for c4 in range(4):
                    cc = c4 * 128
                    mm(bS[:, cc:cc + 128], kin[:, cc:cc + 128], qin[:, cc:cc + 128], True, True, kink + qink, bSk)
                Sm, Smk = scrB.get(512)
                tt("dve", Sm, bS[:, 0:512], masks[:], ALU.mult, bSk + ["masks"], Smk)
                c["Sm"] = (Sm, Smk)
                if n > 0:
                    Q3(c, n)

            def P4a(n):
                c = cx[n]
                qin, qink = c["qin"]
                Sm, Smk = c["Sm"]
                bo, bok = nps()
                for hl in range(2):
                    vv, vk = v_v(n, hl * 256, 256)
                    sf, sfk = Sf_v(n, hl)
                    o_ = bo[:, hl * 256:(hl + 1) * 256]
                    mm(o_, Sm[:, hl * 128:(hl + 1) * 128], vv, True, False, Smk + vk, bok)
                    mm(o_, Sm[:, 256 + hl * 128:256 + (hl + 1) * 128], vv, False, False, Smk + vk, bok)
                    mm(o_, qin[:, hl * 128:(hl + 1) * 128], sf, False, False, qink + sfk, bok)
                    mm(o_, qin[:, 256 + hl * 128:256 + (hl + 1) * 128], SbB[:, cur[0], hl, :], False, True,
                       qink + [("SbB", cur[0])], bok)
                ms, msk = nstat(2)
                for hl in range(2):
                    act(junk[:, 0:256], bo[:, hl * 256:(hl + 1) * 256], AF.Square, bok, msk, scale=1.0 / 16.0,
                        accum=ms[:, hl:hl + 1])
                ln, lnk = nstat(2)
                act(ln, ms, AF.Ln, msk, lnk, bias=EPS)
                rs, rsk = nstat(2)
                act(rs, ln, AF.Exp, lnk, rsk, scale=-0.5)
                c["bo"] = (bo, bok)
                c["rs"] = (rs, rsk)
                if n > 0:
                    Q4(c, n, stB, ["stB"])
                    nxt = 1 - cur[0]
                    cp("pool", SbB[:, nxt].rearrange("p h e -> p (h e)"), stB[:].rearrange("p h e -> p (h e)"),
                       ["stB"], [("SbB", nxt)])
                    cur[0] = nxt

            def P4b(n):
                c = cx[n]
                bo, bok = c["bo"]
                rs, rsk = c["rs"]
                on, onk = scrB.get(512)
                for hl in range(2):
                    stt(on[:, hl * 256:(hl + 1) * 256], bo[:, hl * 256:(hl + 1) * 256], rs[:, hl:hl + 1], glag[:],
                        ALU.mult, ALU.mult, bok + rsk + ["glag"], onk)
                c["on"] = (on, onk)

            def P5(n):
                on, onk = cx[n]["on"]
                tb, tbk = npt()
                for c4 in range(4):
                    tr(tb[:, c4 * 128:(c4 + 1) * 128], on[:, c4 * 128:(c4 + 1) * 128], onk, tbk)
                cp("act", onT[:, hp * 4:(hp + 1) * 4, n * 128:(n + 1) * 128],
                   tb[:, 0:512].rearrange("p (c t) -> p c t", c=4), tbk, [("on", hp * 4 + j, n // 4) for j in range(4)])
                del cx[n]

            yield from pipeline_gen(list(range(ntiles - 1, -1, -1)),
                                    [(5, P4b), (0, P1), (1, P2a), (2, P2b), (3, P3), (4, P4a), (5, P5)], qfill)

    def stage4(hh):
        t0 = hh * 1024
        for jb in range(4):
            def fn(slot, jb=jb):
                for cl in range(2):
                    j = jb * 2 + cl
                    for sl in range(2):
                        s = hh * 2 + sl
                        bank, bk = nps()
                        proj_fm(slot, cl * 128, 128, t0 + sl * 512, 512, bank, bk)
                        sr, srk = scrB.get(512)
                        act(sr, bank[:, 0:512], AF.Silu, bk, srk)
                        o_ = onT[:, j, s * 512:(s + 1) * 512]
                        tt("pool", o_, o_, sr, ALU.mult, [("on", j, s)] + srk, [("on", j, s)])
            items.append((w_in_d[:, C_R + jb * 256:C_R + jb * 256 + 256], fn))

    def stage5(hh):
        t0 = hh * 1024
        th = t0 + 1024 if hh == 0 else t0 - 1
        hcol = 1025 if hh == 0 else 0
        for j in range(8):
            hs = {}

            def fA(slot, j=j, hs=hs):
                for sl in range(2):
                    bank, bk = nps()
                    proj_fm(slot, 0, 128, t0 + sl * 512, 512, bank, bk)
                    cc, cck = scrF.get(512)
                    act(cc, bank[:, 0:512], AF.Copy, bk, cck)
                    hs[("cc", sl)] = (cc, cck)
                if hh == 0:
                    bank, bk = nps()
                    proj_fm(slot, 0, 128, th, 1, bank, bk)
                    cch, cchk = nstat()
                    act(cch, bank[:, 0:1], AF.Copy, bk, cchk)
                    hs["cch"] = (cch, cchk)

            def fB(slot, j=j, hs=hs):
                pv, pk = scrB.get(1026)
                for sl in range(2):
                    bank, bk = nps()
                    proj_fm(slot, 0, 128, t0 + sl * 512, 512, bank, bk)
                    cc, cck = hs[("cc", sl)]
                    tt("dve", pv[:, 1 + sl * 512:1 + (sl + 1) * 512], bank[:, 0:512], cc, ALU.mult, bk + cck, pk)
                if hh == 0:
                    bank, bk = nps()
                    proj_fm(slot, 0, 128, th, 1, bank, bk)
                    cch, cchk = hs["cch"]
                    tt("dve", pv[:, hcol:hcol + 1], bank[:, 0:1], cch, ALU.mult, bk + cchk, pk)
                    cp("pool", pv[:, 0:1], pmeta[:, j:j + 1], ["pmeta"], pk)
                    cp("pool", phalo[:, j:j + 1], pv[:, 1024:1025], pk, [("phalo", j)])
                else:
                    cp("pool", pv[:, 0:1], phalo[:, j:j + 1], [("phalo", j)], pk)
                    mset("pool", pv[:, 1025:1026], 0.0, pk)
                for sl in range(2):
                    c0 = sl * 512
                    tc, tck = scrF.get(512)
                    act(tc, pv[:, 1 + c0:1 + c0 + 512], AF.Copy, pk + ["convw"], tck, scale=convw[:, j, 1:2])
                    stt(tc, pv[:, c0:c0 + 512], convw[:, j, 0:1], tc, ALU.mult, ALU.add, pk + ["convw"] + tck, tck)
                    stt(tc, pv[:, 2 + c0:2 + c0 + 512], convw[:, j, 2:3], tc, ALU.mult, ALU.add, pk + ["convw"] + tck, tck)
                    hs[("tc", sl)] = (tc, tck)

            def fC(slot, j=j, hs=hs):
                for sl in range(2):
                    bank, bk = nps()
                    proj_fm(slot, 0, 128, t0 + sl * 512, 512, bank, bk)
                    sz, szk = scrB.get(512)
                    act(sz, bank[:, 0:512], AF.Silu, bk, szk)
                    hs[("sz", sl)] = (sz, szk)

            def fD(slot, j=j, hs=hs):
                for sl in range(2):
                    bank, bk = nps()
                    proj_fm(slot, 0, 128, t0 + sl * 512, 512, bank, bk)
                    sz, szk = hs[("sz", sl)]
                    tc, tck = hs[("tc", sl)]
                    tb_, tbk_ = scrF.get(512)
                    tt("dve", tb_, bank[:, 0:512], sz, ALU.mult, bk + szk, tbk_)
                    yv, yk = ycT_v(j, sl * 512, 512)
                    tt("pool", yv, tb_, tc, ALU.mult, tbk_ + tck, yk)

            for (c0, f) in ((C_CC, fA), (C_CX, fB), (C_CZ, fC), (C_CB, fD)):
                items.append((w_in_d[:, c0 + j * 128:c0 + (j + 1) * 128], f))

    def stage6(hh):
        t0 = hh * 1024
        for d in range(8):
            st_ = {}

            def fMA(slot, st_=st_):
                for sl in range(2):
                    bank, bk = nps()
                    proj_fm(slot, 0, 128, t0 + sl * 512, 512, bank, bk)
                    sa, sak = scrF.get(512)
                    act(sa, bank[:, 0:512], AF.Sigmoid, bk, sak)
                    st_[("sa", sl)] = (sa, sak)

            def fOC(slot, d=d, st_=st_):
                for sl in range(2):
                    bank, bk = nps()
                    for c in range(8):
                        yv, yk = ycT_v(c, sl * 512, 512)
                        mm(bank[:, 0:512], wbuf[:, slot, c, 0:128], yv, c == 0, c == 7, [("w", slot)] + yk, bk)
                    sa, sak = st_[("sa", sl)]
                    t1, t1k = scrF.get(512)
                    tt("dve", t1, bank[:, 0:512], sa, ALU.mult, bk + sak, t1k)
                    st_[("t1", sl)] = (t1, t1k)

            def fMB(slot, st_=st_):
                for sl in range(2):
                    bank, bk = nps()
                    proj_fm(slot, 0, 128, t0 + sl * 512, 512, bank, bk)
                    sb_, sbk = scrF.get(512)
                    act(sb_, bank[:, 0:512], AF.Sigmoid, bk, sbk)
                    st_[("sb", sl)] = (sb_, sbk)

            def fOG(slot, d=d, st_=st_):
                for sl in range(2):
                    s = hh * 2 + sl
                    bank, bk = nps()
                    for c in range(8):
                        mm(bank[:, 0:512], wbuf[:, slot, c, 0:128], onT[:, c, s * 512:(s + 1) * 512], c == 0, c == 7,
                           [("w", slot), ("on", c, s)], bk)
                    sb_, sbk = st_[("sb", sl)]
                    t2, t2k = scrF.get(512)
                    tt("dve", t2, bank[:, 0:512], sb_, ALU.mult, bk + sbk, t2k)
                    t1, t1k = st_[("t1", sl)]
                    mv, mk = mT_v(d, sl * 512, 512)
                    tt("pool", mv, t1, t2, ALU.add, t1k + t2k, mk)

            items.append((w_in_d[:, C_MA + d * 128:C_MA + (d + 1) * 128], fMA))
            items.append((w_oc_d[:, d * 128:(d + 1) * 128], fOC))
            items.append((w_in_d[:, C_MB + d * 128:C_MB + (d + 1) * 128], fMB))
            items.append((w_og_d[:, d * 128:(d + 1) * 128], fOG))

    def stage7(b, hh):
        def fn(_):
            xs = {}
            if hh == 1 and b + 1 < NB:
                s0pre[b + 1] = stage0_prefetch(lambda i, b=b: x_d[b + 1, i * 128:(i + 1) * 128, :])

            def ldx(il):
                i = hh * 8 + il
                xv, xk = scrF.get(1024)
                dma("sp", xv, x_d[b, i * 128:(i + 1) * 128, :], [], xk, ("xt", xk[0][1]))
                xs[il] = (xv, xk)

            ldx(0)
            for il in range(8):
                i = hh * 8 + il
                if il + 1 < 8:
                    ldx(il + 1)
                banks = []
                for h2 in range(2):
                    bank, bk = nps()
                    for d in range(8):
                        mv, mk = mT_v(d, il * 128, 128)
                        mm(bank[:, 0:512], mv, wo_view[:, d, h2 * 512:(h2 + 1) * 512], d == 0, d == 7, mk + wo_keys, bk)
                    banks.append((bank, bk))
                ms, msk = nstat(2)
                for h2 in range(2):
                    act(junk[:, 0:512], banks[h2][0][:, 0:512], AF.Square, banks[h2][1], msk, scale=1.0 / 32.0,
                        accum=ms[:, h2:h2 + 1])
                m1, m1k = nstat()
                tt("dve", m1, ms[:, 0:1], ms[:, 1:2], ALU.add, msk, m1k)
                ln, lnk = nstat()
                act(ln, m1, AF.Ln, m1k, lnk, bias=EPS)
                rs, rsk = nstat()
                act(rs, ln, AF.Exp, lnk, rsk, scale=-0.5)
                yv, yk = scrF.get(1024)
                for h2 in range(2):
                    stt(yv[:, h2 * 512:(h2 + 1) * 512], banks[h2][0][:, 0:512], rs, gpost[:, h2 * 512:(h2 + 1) * 512],
                        ALU.mult, ALU.mult, banks[h2][1] + rsk + ["gpost"], yk)
                xv, xk = xs.pop(il)
                tt("pool", yv, yv, xv, ALU.add, yk + xk, yk)
                out_dmas.append(dma("sp", y_d[b, i * 128:(i + 1) * 128, :], yv, yk, [], ("y", yk[0][1])))
        items.append((None, fn))

    def load_wo():
        def fn(_):
            src = w_o_d.rearrange("(ko ki) n -> ki ko n", ki=128)
            for h2 in range(2):
                dma("pool", wo_view[:, :, h2 * 512:(h2 + 1) * 512], src[:, :, h2 * 512:(h2 + 1) * 512], [], wo_keys, "wo")
        items.append((None, fn))

    s0pre[0] = stage0_prefetch(lambda i: x_d[0, i * 128:(i + 1) * 128, :])
    stage0(lambda i: metap_d[:, :], 1)
    for hp in range(2):
        stage1(hp, 1, False, hp == 0)
        passF_meta(hp)
    for jb in range(4):
        hs = {}

        def fA(slot, jb=jb, hs=hs):
            bank, bk = nps()
            for cl in range(2):
                for k in range(8):
                    mm(bank[:, cl * 128:(cl + 1) * 128], wbuf[:, slot, k, cl * 128:(cl + 1) * 128], uT[:, k, 0:128],
                       k == 0, k == 7, [("w", slot), ("uT", 0)], bk)
            cc, cck = scrF.get(256)
            act(cc, bank[:, 0:256], AF.Copy, bk, cck)
            hs["cc"] = (cc, cck)

        def fB(slot, jb=jb, hs=hs):
            bank, bk = nps()
            for cl in range(2):
                for k in range(8):
                    mm(bank[:, cl * 128:(cl + 1) * 128], wbuf[:, slot, k, cl * 128:(cl + 1) * 128], uT[:, k, 0:128],
                       k == 0, k == 7, [("w", slot), ("uT", 0)], bk)
            cc, cck = hs["cc"]
            for cl in range(2):
                j = jb * 2 + cl
                tt("dve", pmeta[:, j:j + 1], bank[:, cl * 128 + 127:cl * 128 + 128], cc[:, cl * 128 + 127:cl * 128 + 128],
                   ALU.mult, bk + cck, ["pmeta"])
        items.append((w_in_d[:, C_CC + jb * 256:C_CC + (jb + 1) * 256], fA))
        items.append((w_in_d[:, C_CX + jb * 256:C_CX + (jb + 1) * 256], fB))
    run_items()

    for b in range(NB):
        stage0(lambda i, b=b: x_d[b, i * 128:(i + 1) * 128, :], 16, pre=s0pre.get(b))
        for hp in range(2):
            stage1(hp, 16, True, hp == 0)
            gla_pair(hp, 16)
        load_wo()
        for hh in range(2):
            stage4(hh)
            stage5(hh)
            stage6(hh)
            stage7(b, hh)
        run_items()

    S.add("sp", None, [], [], extra=out_dmas)

    S.fastv = fastv
    S.finalize()
    sems = {}
    for e in ("pe", "act", "dve", "pool", "sp"):
        sems[("eng", e)] = es.enter_context(nc.semaphore(f"s_{e}"))
    for i, k in enumerate(S.dma_keys):
        sems[("dma", k)] = es.enter_context(nc.semaphore(f"d_{i}"))

    with nc.Block() as block:
        @block.tensor
        def _(e):
            S.emit("pe", e, sems)

        @block.scalar
        def _(e):
            S.emit("act", e, sems)

        @block.vector
        def _(e):
            S.emit("dve", e, sems)

        @block.gpsimd
        def _(e):
            S.emit("pool", e, sems)

        @block.sync
        def _(e):
            S.emit("sp", e, sems)
    es.close()
    return nc, S


def _consts():
    s = np.arange(128)[:, None]
    t = np.arange(128)[None, :]
    g = np.float32(-1.0 / 16.0)
    tri = np.zeros((128, 2, 128), np.float32)
    tri[:, 0, :] = np.where(s <= t, g, 0.0)
    tri[:, 1, :] = np.where(s >= t, g, 0.0)
    trid = np.full((128, 2, 130), g, np.float32)
    trid[:, 0, 0:128] = np.where(s > t, g, 0.0)
    trid[:, 1, 0:128] = np.where(s < t, g, 0.0)
    masks = np.zeros((128, 2, 2, 128), np.float32)
    masks[:, 0, :, :] = np.where(s <= t, 1.0, 0.0)[:, None, :]
    masks[:, 1, :, :] = np.where(s > t, 1.0, 0.0)[:, None, :]
    ident = np.eye(128, dtype=np.float32).astype(ml_dtypes.bfloat16)
    return tri, trid, masks.reshape(128, 512), ident


def make_in_maps(inputs, n_cores, NB):
    f = lambda a: np.ascontiguousarray(np.asarray(a, dtype=np.float32))
    x = f(inputs["x"])
    tri, trid, masks, ident = _consts()
    metap = np.zeros((128, D), np.float32)
    metap[112:128] = f(inputs["meta_tokens"])
    wg = np.zeros((33, 2, 512), np.float32)
    wg[0:16, 0] = f(inputs["w_gate_fwd"])[0]
    wg[32, 0] = f(inputs["b_gate_fwd"])[0]
    wg[16:32, 1] = f(inputs["w_gate_bwd"])[0]
    wg[32, 1] = f(inputs["b_gate_bwd"])[0]
    convw = np.ascontiguousarray(f(inputs["conv_w"])[0].T.reshape(8, 128, 3).transpose(1, 0, 2))
    common = {
        "metap": metap,
        "w_in": f(inputs["w_in"])[0],
        "w_oc": f(inputs["w_out_conv"])[0],
        "w_og": f(inputs["w_out_gla"])[0],
        "w_o": f(inputs["w_merge_out"])[0],
        "gpre_bc": np.ascontiguousarray(np.broadcast_to(f(inputs["norm_pre"])[0][None, :], (128, D))),
        "gpost_bc": np.ascontiguousarray(np.broadcast_to(f(inputs["norm_post"])[0][None, :], (128, D))),
        "glag_bc": np.ascontiguousarray(np.broadcast_to(f(inputs["gla_norm"])[0][None, :], (128, 256))),
        "ident": ident, "tri": tri, "trid": trid, "masks": masks, "wg": wg, "convw": convw,
    }
    maps = []
    for c in range(n_cores):
        m = dict(common)
        m["x"] = np.ascontiguousarray(x[c * NB:(c + 1) * NB])
        maps.append(m)
    return maps


_CACHE = {}


def kernel(**inputs):
    NB = inputs["x"].shape[0] // N_CORES
    if NB not in _CACHE:
        _CACHE[NB] = build_program(NB)[0]
    nc = _CACHE[NB]
    in_maps = make_in_maps(inputs, N_CORES, NB)
    res = run_bass_kernel_spmd(nc, in_maps, core_ids=list(range(N_CORES)))
    out = np.concatenate([np.asarray(r["y"]) for r in res.results], axis=0)
    return out.astype(np.float32)
```

```python
import numpy as np
import ml_dtypes
from contextlib import ExitStack

import concourse.bass as bass
import concourse.mybir as mybir
from concourse.bass_utils import run_bass_kernel_spmd

F32 = mybir.dt.float32
BF16 = mybir.dt.bfloat16
F32R = mybir.dt.float32r
AF = mybir.ActivationFunctionType
ALU = mybir.AluOpType

N_CORES = 8
D = 1024
SEQ = 2048
N_IN = 9248
EPS = 1e-6
C_CB, C_CC, C_CX, C_CZ = 0, 1024, 2048, 3072
C_Q, C_K, C_V, C_R, C_LR, C_MA, C_MB = 4096, 4608, 5120, 6144, 7168, 7200, 8224


class Op:
    __slots__ = ("eng", "fn", "deps", "sig", "need", "semkey", "idx")

    def __init__(self, eng, fn, semkey=None):
        self.eng = eng
        self.fn = fn
        self.deps = set()
        self.sig = None
        self.need = False
        self.semkey = semkey
        self.idx = None


class Sched:
    ENGS = ("pe", "act", "dve", "pool", "sp")

    def __init__(self):
        self.ops = {e: [] for e in self.ENGS}
        self.last_w = {}
        self.readers = {}
        self.shared_all = set()

    def add(self, eng, fn, reads=(), writes=(), semkey=None, extra=()):
        op = Op(eng, fn, semkey)
        _check_keys(reads)
        _check_keys(writes)
        deps = op.deps
        deps.update(extra)
        for k in reads:
            w = self.last_w.get(k)
            if w is not None:
                deps.add(w)
        for k in writes:
            w = self.last_w.get(k)
            if w is not None:
                deps.add(w)
            for r in self.readers.get(k, ()):
                deps.add(r)
        deps.discard(op)
        if eng == "pe":
            op.deps = {d for d in deps if not (d.eng == "pe" and d.semkey is None)}
        for k in reads:
            self.readers.setdefault(k, []).append(op)
        for k in writes:
            self.last_w[k] = op
            self.readers[k] = []
        op.idx = len(self.ops[eng])
        self.ops[eng].append(op)
        return op

    def finalize(self):
        for e in self.ENGS:
            for op in self.ops[e]:
                best = {}
                keep = set()
                for d in op.deps:
                    if d.semkey is not None:
                        keep.add(d)
                    elif d.eng not in best or d.idx > best[d.eng].idx:
                        best[d.eng] = d
                keep.update(best.values())
                op.deps = keep
                for d in op.deps:
                    d.need = True
        self.nsig = {}
        for e in self.ENGS:
            c = 0
            for op in self.ops[e]:
                if op.semkey is None and op.need:
                    c += 1
                    op.sig = (("eng", e), c)
        cnt = {}
        dma_ops = []
        for e in self.ENGS:
            for op in self.ops[e]:
                if op.semkey is not None:
                    cnt[op.semkey] = cnt.get(op.semkey, 0) + 1
                    op.sig = (("dma", op.semkey), 16 * cnt[op.semkey])
                    dma_ops.append(op)
        for op in dma_ops:
            if op.semkey in self.shared_all:
                op.sig = (("dma", op.semkey), 16 * cnt[op.semkey])
        self.dma_keys = list(cnt.keys())

    def emit(self, eng, e, sems):
        waited = {}
        for op in self.ops[eng]:
            need = {}
            for d in op.deps:
                s, v = d.sig
                if v > need.get(s, 0):
                    need[s] = v
            for s, v in need.items():
                if waited.get(s, 0) >= v:
                    continue
                e.wait_ge(sems[s], v)
                waited[s] = v
            if op.fn is None:
                continue
            ins = op.fn(e)
            if op.semkey is not None:
                ins.then_inc(sems[op.sig[0]], 16)
            elif op.need:
                ins.then_inc(sems[op.sig[0]], 1)


class RKey(tuple):
    gen = 0
    ring = None


class Ring:
    def __init__(self, name, tensor, slot_elems, nslots):
        self.name = name
        self.t = tensor
        self.se = slot_elems
        self.n = nslots
        self.pos = 0
        self.gen = [0] * nslots

    def get(self, nelems):
        k = -(-nelems // self.se)
        if self.pos + k > self.n:
            self.pos = 0
        s0 = self.pos
        self.pos += k
        keys = []
        for sl in range(s0, s0 + k):
            self.gen[sl] += 1
            rk = RKey((self.name, sl))
            rk.gen = self.gen[sl]
            rk.ring = self
            keys.append(rk)
        return self.t[:, s0 * self.se:s0 * self.se + nelems], keys


def _check_keys(keys):
    for k in keys:
        if isinstance(k, RKey) and k.ring.gen[k[1]] != k.gen:
            raise RuntimeError(f"stale ring slot use {tuple(k)} gen {k.gen} != {k.ring.gen[k[1]]}")


def build_program(NB):
    nc = bass.Bass("TRN2", target_bir_lowering=False)
    S = Sched()

    def din(name, shape, dt=F32):
        return nc.dram_tensor(name, list(shape), dt, kind="ExternalInput").ap()

    x_d = din("x", [NB, SEQ, D])
    metap_d = din("metap", [128, D])
    w_in_d = din("w_in", [D, N_IN])
    w_oc_d = din("w_oc", [D, D])
    w_og_d = din("w_og", [D, D])
    w_o_d = din("w_o", [D, D])
    gpre_d = din("gpre_bc", [128, D])
    gpost_d = din("gpost_bc", [128, D])
    glag_d = din("glag_bc", [128, 256])
    ident_d = din("ident", [128, 128], BF16)
    tri_d = din("tri", [128, 2, 128])
    trid_d = din("trid", [128, 2, 130])
    masks_d = din("masks", [128, 512])
    wg_d = din("wg", [33, 2, 512])
    convw_d = din("convw", [128, 8, 3])
    y_d = nc.dram_tensor("y", [NB, SEQ, D], F32, kind="ExternalOutput").ap()

    es = ExitStack()

    def sb(name, shape, dt):
        return es.enter_context(nc.sbuf_tensor("sb_" + name, list(shape), dt))

    uT = sb("uT", [128, 8, SEQ], BF16)
    onT = sb("onT", [128, 8, SEQ], BF16)
    G = sb("G", [128, 24576], BF16)
    lrT = sb("lrT", [33, SEQ], BF16)
    wbuf = sb("wbuf", [128, 4, 8, 256], BF16)
    junk = sb("junk", [128, D], BF16)
    NSF, NSB, NSQ, NSP = 9, 16, 7, 3
    scrF_t = sb("scrF", [128, NSF * 512], F32)
    scrB_t = sb("scrB", [128, NSB * 512], BF16)
    scrQ_t = sb("scrQ", [128, NSQ * 512], BF16)
    scrP_t = sb("scrP", [128, NSP * 512], F32R)
    stF = sb("stF", [128, 2, 256], F32)
    stB = sb("stB", [128, 2, 256], F32)
    SbB = sb("SbB", [128, 2, 2, 256], BF16)
    stat = sb("stat", [128, 128], F32)
    gpre = sb("gpre", [128, D], F32)
    gpost = sb("gpost", [128, D], F32)
    glag = sb("glag", [128, 256], F32)
    ident = sb("ident", [128, 128], BF16)
    tri = sb("tri", [128, 2, 128], F32R)
    trid = sb("trid", [128, 2, 130], F32R)
    masks = sb("masks", [128, 512], F32)
    wg = sb("wg", [33, 2, 512], BF16)
    convw = sb("convw", [128, 8, 3], F32)
    Smeta = sb("Smeta", [128, 4, 256], F32)
    pmeta = sb("pmeta", [128, 8], BF16)
    phalo = sb("phalo", [128, 8], BF16)

    ps = [es.enter_context(nc.psum_tensor(f"ps{i}", [128, 512], F32)) for i in range(6)]
    pt = [es.enter_context(nc.psum_tensor(f"pt{i}", [128, 1024], BF16)) for i in range(2)]

    scrF = Ring("scrF", scrF_t, 512, NSF)
    scrB = Ring("scrB", scrB_t, 512, NSB)
    scrQ = Ring("scrQ", scrQ_t, 512, NSQ)
    scrP = Ring("scrP", scrP_t, 512, NSP)

    def gkeys(lo, hi):
        return [("G", g) for g in range(lo // 512, (hi - 1) // 512 + 1)]

    def qT_v(hl, t0, n):
        o = hl * 2048 + t0
        return G[:, o:o + n], gkeys(o, o + n)

    def kT_v(hl, t0, n):
        o = 4096 + hl * 2048 + t0
        return G[:, o:o + n], gkeys(o, o + n)

    def qk_tile(base, n):
        v = G[:, base:base + 4096].rearrange("p (h t) -> p h t", h=2)[:, :, n * 128:(n + 1) * 128]
        ks = gkeys(base + n * 128, base + n * 128 + 128) + gkeys(base + 2048 + n * 128, base + 2048 + n * 128 + 128)
        return v, ks

    def v_v(i, c0, n):
        o = 8192 + i * 512 + c0
        return G[:, o:o + n], gkeys(o, o + n)

    def Sf_v(n, hl):
        o = 16384 + n * 512 + hl * 256
        return G[:, o:o + 256], gkeys(o, o + 256)

    def Sf_tile(n):
        o = 16384 + n * 512
        return G[:, o:o + 512], gkeys(o, o + 512)

    def ycT_v(j, tl0, n):
        o = j * 1024 + tl0
        return G[:, o:o + n], gkeys(o, o + n)

    def mT_v(d, tl0, n):
        o = 8192 + d * 1024 + tl0
        return G[:, o:o + n], gkeys(o, o + n)

    WO0 = 16384
    wo_view = G[:, WO0:WO0 + 8192].rearrange("p (k n) -> p k n", k=8)
    wo_keys = gkeys(WO0, WO0 + 8192)

    def uT_keys(t0, n):
        return [("uT", i) for i in range(t0 // 128, (t0 + n - 1) // 128 + 1)]

    cnt = {"ps": 0, "pt": 0, "st": 0, "w": 0}

    def nps():
        i = cnt["ps"] % 6
        cnt["ps"] += 1
        return ps[i], [("ps", i)]

    def npt():
        i = cnt["pt"] % 2
        cnt["pt"] += 1
        return pt[i], [("pt", i)]

    def nstat(n=1):
        i = cnt["st"] % 32
        cnt["st"] += 1
        return stat[:, i * 4:i * 4 + n], [("st", i)]

    def mm(out, lhsT, rhs, start, stop, r, w):
        S.add("pe", lambda e: e.matmul(out, lhsT=lhsT, rhs=rhs, start=start, stop=stop), r, w)

    def tr(out, in_, r, w):
        S.add("pe", lambda e: e.transpose(out, in_, ident[:]), r + ["ident"], w)

    def act(out, in_, func, r, w, bias=None, scale=None, accum=None):
        kw = {}
        if bias is not None:
            kw["bias"] = bias
        if scale is not None:
            kw["scale"] = scale
        if accum is not None:
            kw["accum_out"] = accum
        S.add("act", lambda e: e.activation(out=out, in_=in_, func=func, **kw), r, w)

    def tt(eng, out, a, b, op, r, w):
        S.add(eng, lambda e: e.tensor_tensor(out=out, in0=a, in1=b, op=op), r, w)

    def stt(out, in0, scalar, in1, op0, op1, r, w):
        S.add("dve", lambda e: e.scalar_tensor_tensor(out=out, in0=in0, scalar=scalar, in1=in1, op0=op0, op1=op1), r, w)

    def cp(eng, out, in_, r, w):
        if eng == "act":
            S.add(eng, lambda e: e.activation(out=out, in_=in_, func=AF.Copy), r, w)
        else:
            S.add(eng, lambda e: e.tensor_copy(out=out, in_=in_), r, w)

    def mset(eng, ap, val, w):
        S.add(eng, lambda e: e.memset(ap, val), [], w)

    def dma(q, out, in_, r, w, semkey):
        return S.add(q, lambda e: e.dma_start(out=out, in_=in_), r, w, semkey=semkey)

    out_dmas = []

    S.shared_all.add("c")
    for (dst, src, key, q) in [
        (gpre[:], gpre_d, "gpre", "sp"), (gpost[:], gpost_d, "gpost", "sp"), (glag[:], glag_d, "glag", "sp"),
        (ident[:], ident_d, "ident", "sp"),
        (masks[:], masks_d, "masks", "sp"), (convw[:], convw_d, "convw", "sp"),
    ]:
        dma(q, dst, src, [], [key], "c")
    t1_, t1k_ = scrF.get(256)
    dma("sp", t1_.rearrange("p (a b) -> p a b", a=2), tri_d, [], t1k_, "c")
    cp("dve", tri[:], t1_.rearrange("p (a b) -> p a b", a=2), t1k_, ["tri"])
    t2_, t2k_ = scrF.get(260)
    dma("sp", t2_.rearrange("p (a b) -> p a b", a=2), trid_d, [], t2k_, "c")
    cp("dve", trid[:], t2_.rearrange("p (a b) -> p a b", a=2), t2k_, ["trid"])
    dma("pool", wg[:], wg_d, [], ["wg"], "cw")
    mset("pool", lrT[32:33, :], 1.0, ["lr1"])

    items = []

    def wload(spec):
        slot = cnt["w"] % 4
        cnt["w"] += 1
        ncols = spec.shape[1]
        src = spec.rearrange("(ko ki) n -> ki ko n", ki=128)
        dma("pool", wbuf[:, slot, :, 0:ncols], src, [], [("w", slot)], ("w", slot))
        return slot

    def run_items():
        blocks = []
        for i, (spec, fn) in enumerate(items):
            if spec is None:
                continue
            for pos, sp_ in enumerate(spec if isinstance(spec, list) else [spec]):
                blocks.append([i, pos, sp_])
        slots = {}
        issued = 0
        consumed = 0
        for i, (spec, fn) in enumerate(items):
            nmine = 0 if spec is None else (len(spec) if isinstance(spec, list) else 1)
            while issued < len(blocks) and (issued - consumed < 4 or blocks[issued][0] <= i):
                j, pos, sp_ = blocks[issued]
                if pos == 0 and isinstance(items[j][0], list) and len(items[j][0]) == 3 and cnt["w"] % 4 == 2:
                    b0, b1, b2 = blocks[issued], blocks[issued + 1], blocks[issued + 2]
                    blocks[issued], blocks[issued + 1], blocks[issued + 2] = b1, b2, b0
                    j, pos, sp_ = blocks[issued]
                slots.setdefault(j, {})[pos] = wload(sp_)
                issued += 1
            if spec is None:
                fn(None)
            elif isinstance(spec, list):
                fn([slots[i][p] for p in range(len(spec))])
            else:
                fn(slots[i][0])
            consumed += nmine
        items.clear()

    def proj_fm(slot, coff, M, t0, ntok, bank, bkeys):
        for k in range(8):
            mm(bank[0:M, 0:ntok], wbuf[:, slot, k, coff:coff + M], uT[:, k, t0:t0 + ntok],
               k == 0, k == 7, [("w", slot)] + uT_keys(t0, ntok), bkeys)

    def stage0(src_tile_ap_fn, ntiles, pre=None):
        cx = dict(pre) if pre else {}
        done_A = set(cx.keys())

        def A(i):
            if i in done_A:
                return
            if ntiles == 1:
                xv, xk = scrF.get(1024)
                dma("sp", xv, src_tile_ap_fn(i), [], xk, ("xt", xk[0][1]))
            else:
                j = i % 8
                xv = onT[:, j, :].bitcast(F32)
                xk = [("on", j, s4) for s4 in range(4)]
                dma("sp", xv, src_tile_ap_fn(i), [], xk, ("xt", "on", j))
            cx[i] = {"x": (xv, xk)}

        def B(i):
            xv, xk = cx[i]["x"]
            ms, msk = nstat()
            act(junk[:], xv, AF.Square, xk, msk, scale=1.0 / 32.0, accum=ms)
            ln, lnk = nstat()
            act(ln, ms, AF.Ln, msk, lnk, bias=EPS)
            rs, rsk = nstat()
            act(rs, ln, AF.Exp, lnk, rsk, scale=-0.5)
            cx[i]["rs"] = (rs, rsk)

        def C(i):
            xv, xk = cx[i]["x"]
            rs, rsk = cx[i]["rs"]
            us, usk = scrB.get(1024)
            stt(us, xv, rs, gpre[:], ALU.mult, ALU.mult, xk + ["gpre"] + rsk, usk)
            cx[i]["us"] = (us, usk)

        def Dd(i):
            us, usk = cx[i]["us"]
            tb, tbk = npt()
            for k in range(8):
                tr(tb[:, k * 128:(k + 1) * 128], us[:, k * 128:(k + 1) * 128], usk, tbk)
            cp("dve", uT[:, :, i * 128:(i + 1) * 128], tb[:, 0:1024].rearrange("p (k t) -> p k t", k=8),
               tbk, [("uT", i)])
            del cx[i]

        phases = [(5, C), (6, Dd), (0, A), (4, B)]
        for it in range(ntiles + 6):
            for lag, ph in phases:
                i = it - lag
                if 0 <= i < ntiles:
                    ph(i)

    def stage0_prefetch(src_tile_ap_fn, n=8):
        pre = {}
        for i in range(n):
            j = i % 8
            xv = onT[:, j, :].bitcast(F32)
            xk = [("on", j, s4) for s4 in range(4)]
            dma("sp", xv, src_tile_ap_fn(i), [], xk, ("xt", "on", j))
            pre[i] = {"x": (xv, xk)}
        return pre

    s0pre = {}

    def stage1(hp, ntiles, need_q, need_lr):
        T = ntiles * 128
        sts = [(s * 512, min(512, T - s * 512)) for s in range((T + 511) // 512)]

        def f_qk(base_view, scale):
            def fn(slot):
                for hl in range(2):
                    for (t0, n) in sts:
                        bank, bk = nps()
                        proj_fm(slot, hl * 128, 128, t0, n, bank, bk)
                        dst, dk = base_view(hl, t0, n)
                        act(dst, bank[:, 0:n], AF.Copy, bk, dk, scale=scale)
            return fn

        items.append((w_in_d[:, C_K + hp * 256:C_K + hp * 256 + 256], f_qk(kT_v, 1.0)))

        def f_v(blk):
            def fn(slot):
                for i in range(ntiles):
                    bank, bk = nps()
                    for k in range(8):
                        mm(bank[:, 0:256], uT[:, k, i * 128:(i + 1) * 128], wbuf[:, slot, k, 0:256],
                           k == 0, k == 7, [("w", slot), ("uT", i)], bk)
                    dst, dk = v_v(i, blk * 256, 256)
                    cp("dve", dst, bank[:, 0:256], bk, dk)
            return fn

        if ntiles == 1:
            for blk in range(2):
                c0 = C_V + hp * 512 + blk * 256
                items.append((w_in_d[:, c0:c0 + 256], f_v(blk)))

        if need_lr:
            def f_lr(slot):
                for (t0, n) in sts:
                    bank, bk = nps()
                    proj_fm(slot, 0, 32, t0, n, bank, bk)
                    act(lrT[0:32, t0:t0 + n], bank[0:32, 0:n], AF.Copy, bk, [("lr", t0 // 512)])
            items.append((w_in_d[:, C_LR:C_LR + 32], f_lr))

    def gate_sp(n, hp, dirs):
        nd = len(dirs)
        bank, bk = nps()
        for di, d in enumerate(dirs):
            mm(bank[:, di * 256:(di + 1) * 256], lrT[0:33, n * 128:(n + 1) * 128], wg[0:33, d, hp * 256:(hp + 1) * 256],
               True, True, [("lr", n // 4), "lr1", "wg"], bk)
        e1, e1k = scrF.get(512)
        act(e1[:, 0:nd * 256], bank[:, 0:nd * 256], AF.Exp, bk, e1k, scale=-1.0)
        sp, spk = scrP.get(512)
        act(sp[:, 0:nd * 256], e1[:, 0:nd * 256], AF.Ln, e1k, spk, bias=1.0)
        return sp, spk

    def Q2a(c, n, spoff, d):
        sp, spk = c["sp"]
        bank, bk = nps()
        for hl in range(2):
            mm(bank[:, hl * 130:(hl + 1) * 130], sp[:, spoff + hl * 128:spoff + (hl + 1) * 128], trid[:, d, :],
               True, True, spk + ["trid"], bk)
        b3 = bank[:, 0:260].rearrange("p (h t) -> p h t", h=2)
        ed, edk = scrB.get(256)
        ed3 = ed.rearrange("p (h t) -> p h t", h=2)
        act(ed3, b3[:, :, 0:128], AF.Exp, bk, edk)
        dec, deck = nstat(2)
        act(dec.rearrange("p (h o) -> p h o", o=1), b3[:, :, 128:129], AF.Exp, bk, deck)
        c["dec"] = (dec, deck)
        c["ed"] = (ed3, edk)

    def Q2b(c, n):
        ed3, edk = c["ed"]
        kt, ktk = qk_tile(4096, n)
        kdT, kdTk = scrB.get(256)
        tt("dve", kdT.rearrange("p (h t) -> p h t", h=2), kt, ed3, ALU.mult, ktk + edk, kdTk)
        c["kdT"] = (kdT, kdTk)

    def Q3(c, n):
        kdT, kdTk = c["kdT"]
        tb, tbk = npt()
        for hl in range(2):
            tr(tb[:, hl * 128:(hl + 1) * 128], kdT[:, hl * 128:(hl + 1) * 128], kdTk, tbk)
        kd, kdk = scrB.get(256)
        cp("dve", kd, tb[:, 0:256], tbk, kdk)
        c["kd"] = (kd, kdk)

    def Q4(c, n, st, stk):
        kd, kdk = c["kd"]
        dec, deck = c["dec"]
        bank2, bk2 = nps()
        for hl in range(2):
            vv, vk = v_v(n, hl * 256, 256)
            mm(bank2[:, hl * 256:(hl + 1) * 256], kd[:, hl * 128:(hl + 1) * 128], vv, True, True, kdk + vk, bk2)
        for hl in range(2):
            stt(st[:, hl, :], st[:, hl, :], dec[:, hl:hl + 1], bank2[:, hl * 256:(hl + 1) * 256],
                ALU.mult, ALU.add, stk + deck + bk2, stk)

    def pipeline_gen(order, phases, filler=None):
        ph = [(p if isinstance(p, tuple) else (i, p)) for i, p in enumerate(phases)]
        nt = len(order)
        maxlag = max(l for l, _ in ph)
        for it in range(nt + maxlag):
            for lag, fn in ph:
                i = it - lag
                if 0 <= i < nt:
                    fn(order[i])
            if filler is not None:
                filler(it)
            yield it
        if filler is not None:
            filler(None)

    fastv = []

    def passF_gen(hp, ntiles, meta, wslots):
        if True:
            qgroups = []
            vtiles = []
            if not meta:
                qslot, vslot0, vslot1 = wslots
                fastv.append(vslot1 == vslot0 + 1)
                vtiles = list(range(ntiles))

            def emit_v(i):
                if vslot1 == vslot0 + 1:
                    bank, bk = nps()
                    for k in range(8):
                        mm(bank[:, 0:512].rearrange("p (a b) -> p a b", a=2), uT[:, k, i * 128:(i + 1) * 128],
                           wbuf[:, vslot0:vslot0 + 2, k, 0:256], k == 0, k == 7,
                           [("w", vslot0), ("w", vslot1), ("uT", i)], bk)
                    dst, dk = v_v(i, 0, 512)
                    cp("dve", dst, bank[:, 0:512], bk, dk)
                    return
                for blk, vs in ((0, vslot0), (1, vslot1)):
                    bank, bk = nps()
                    for k in range(8):
                        mm(bank[:, 0:256], uT[:, k, i * 128:(i + 1) * 128], wbuf[:, vs, k, 0:256],
                           k == 0, k == 7, [("w", vs), ("uT", i)], bk)
                    dst, dk = v_v(i, blk * 256, 256)
                    cp("dve", dst, bank[:, 0:256], bk, dk)

            def emit_q(g):
                hl, t0, n = g
                bank, bk = nps()
                proj_fm(qslot, hl * 128, 128, t0, n, bank, bk)
                dst, dk = qT_v(hl, t0, n)
                act(dst, bank[:, 0:n], AF.Copy, bk, dk, scale=128.0 ** -0.5)

            def filler(it):
                if it is None:
                    while vtiles:
                        emit_v(vtiles.pop(0))
                    while qgroups:
                        emit_q(qgroups.pop(0))
                    return
                if vtiles:
                    emit_v(vtiles.pop(0))
                if it % 2 == 1 and qgroups:
                    emit_q(qgroups.pop(0))

            if not meta:
                emit_v(vtiles.pop(0))
                emit_v(vtiles.pop(0))

            if meta:
                mset("pool", stF[:], 0.0, ["stF"])
            else:
                cp("pool", stF[:], Smeta[:, hp * 2:(hp + 1) * 2, :], ["Smeta"], ["stF"])
                d0, d0k = Sf_tile(0)
                cp("pool", d0, Smeta[:, hp * 2:(hp + 1) * 2, :].rearrange("p h e -> p (h e)"), ["Smeta"], d0k)
            last = ntiles if meta else ntiles - 1
            cx = {}

            def F1(n):
                cx[n] = {"sp": gate_sp(n, hp, [0])}

            def F2(n):
                Q2a(cx[n], n, 0, 0)

            def F2b(n):
                Q2b(cx[n], n)

            def F3(n):
                Q3(cx[n], n)

            def F4(n):
                Q4(cx[n], n, stF, ["stF"])
                if not meta:
                    dn, dnk = Sf_tile(n + 1)
                    cp("act", dn, stF[:].rearrange("p h e -> p (h e)"), ["stF"], dnk)
                del cx[n]

            yield from pipeline_gen(list(range(last)), [F1, F2, F2b, F3, F4], filler)
            if meta:
                cp("pool", Smeta[:, hp * 2:(hp + 1) * 2, :], stF[:], ["stF"], ["Smeta"])

    def passF_meta(hp):
        def fn(_):
            for _it in passF_gen(hp, 1, True, None):
                pass
        items.append((None, fn))

    def gla_pair(hp, ntiles, overlap=3):
        def fn(wslots):
            gF = passF_gen(hp, ntiles, False, wslots)
            gB = passB_gen(hp, ntiles, wslots[0])
            nF = (ntiles - 1) + 4
            for _ in range(nF - overlap):
                next(gF)
            for _ in range(overlap):
                next(gF)
                next(gB)
            for _it in gF:
                pass
            for _it in gB:
                pass
        cv = C_V + hp * 512
        items.append(([w_in_d[:, C_Q + hp * 256:C_Q + hp * 256 + 256], w_in_d[:, cv:cv + 256],
                       w_in_d[:, cv + 256:cv + 512]], fn))

    def passB_gen(hp, ntiles, qslot):
        if True:
            qsched = {0: (0, 3), 1: (1, 3), 3: (0, 2), 4: (1, 2), 7: (0, 1), 8: (1, 1), 11: (0, 0), 12: (1, 0)}

            def qfill(it):
                if it in qsched:
                    hl, s4 = qsched[it]
                    bank, bk = nps()
                    proj_fm(qslot, hl * 128, 128, s4 * 512, 512, bank, bk)
                    dst, dk = qT_v(hl, s4 * 512, 512)
                    act(dst, bank[:, 0:512], AF.Copy, bk, dk, scale=128.0 ** -0.5)

            mset("pool", stB[:], 0.0, ["stB"])
            mset("pool", SbB[:, 0], 0.0, [("SbB", 0)])
            cur = [0]
            cx = {}

            def P1(n):
                cx[n] = {"sp": gate_sp(n, hp, [0, 1])}

            def P2a(n):
                c = cx[n]
                sp, spk = c["sp"]
                bb, bbk = nps()
                for d in range(2):
                    for hl in range(2):
                        cc = (d * 2 + hl) * 128
                        mm(bb[:, cc:cc + 128], sp[:, cc:cc + 128], tri[:, d, :], True, True, spk + ["tri"], bbk)
                E, Ek = scrB.get(512)
                act(E, bb[:, 0:512], AF.Exp, bbk, Ek)
                Ei, Eik = scrB.get(512)
                act(Ei, bb[:, 0:512], AF.Exp, bbk, Eik, scale=-1.0)
                c["E"] = (E, Ek)
                c["Ei"] = (Ei, Eik)
                if n > 0:
                    Q2a(c, n, 256, 1)

            def P2b(n):
                c = cx[n]
                E, Ek = c["E"]
                Ei, Eik = c["Ei"]
                qt, qtk = qk_tile(0, n)
                kt, ktk = qk_tile(4096, n)
                qin, qink = scrQ.get(512)
                kin, kink = scrQ.get(512)
                for d in range(2):
                    tt("dve", qin[:, d * 256:(d + 1) * 256].rearrange("p (h t) -> p h t", h=2), qt,
                       E[:, d * 256:(d + 1) * 256].rearrange("p (h t) -> p h t", h=2), ALU.mult, qtk + Ek, qink)
                    tt("dve", kin[:, d * 256:(d + 1) * 256].rearrange("p (h t) -> p h t", h=2), kt,
                       Ei[:, d * 256:(d + 1) * 256].rearrange("p (h t) -> p h t", h=2), ALU.mult, ktk + Eik, kink)
                c["qin"] = (qin, qink)
                c["kin"] = (kin, kink)
                if n > 0:
                    Q2b(c, n)

            def P3(n):
                c = cx[n]
                qin, qink = c["qin"]
                kin, kink = c["kin"]
                bS, bSk = nps()
                for c4 in range(4):
                    cc = c4 * 128
                    mm(bS[:, cc:cc + 128], kin[:, cc:cc + 128], qin[:, cc:cc + 128], True, True, kink + qink, bSk)
                Sm, Smk = scrB.get(512)
                tt("dve", Sm, bS[:, 0:512], masks[:], ALU.mult, bSk + ["masks"], Smk)
                c["Sm"] = (Sm, Smk)
                if n > 0:
                    Q3(c, n)

            def P4a(n):
                c = cx[n]
                qin, qink = c["qin"]
                Sm, Smk = c["Sm"]
                bo, bok = nps()
                for hl in range(2):
                    vv, vk = v_v(n, hl * 256, 256)
                    sf, sfk = Sf_v(n, hl)
                    o_ = bo[:, hl * 256:(hl + 1) * 256]
                    mm(o_, Sm[:, hl * 128:(hl + 1) * 128], vv, True, False, Smk + vk, bok)
                    mm(o_, Sm[:, 256 + hl * 128:256 + (hl + 1) * 128], vv, False, False, Smk + vk, bok)
                    mm(o_, qin[:, hl * 128:(hl + 1) * 128], sf, False, False, qink + sfk, bok)
                    mm(o_, qin[:, 256 + hl * 128:256 + (hl + 1) * 128], SbB[:, cur[0], hl, :], False, True,
                       qink + [("SbB", cur[0])], bok)
                ms, msk = nstat(2)
                for hl in range(2):
                    act(junk[:, 0:256], bo[:, hl * 256:(hl + 1) * 256], AF.Square, bok, msk, scale=1.0 / 16.0,
                        accum=ms[:, hl:hl + 1])
                ln, lnk = nstat(2)
                act(ln, ms, AF.Ln, msk, lnk, bias=EPS)
                rs, rsk = nstat(2)
                act(rs, ln, AF.Exp, lnk, rsk, scale=-0.5)
                c["bo"] = (bo, bok)
                c["rs"] = (rs, rsk)
                if n > 0:
                    Q4(c, n, stB, ["stB"])
                    nxt = 1 - cur[0]
                    cp("pool", SbB[:, nxt].rearrange("p h e -> p (h e)"), stB[:].rearrange("p h e -> p (h e)"),
                       ["stB"], [("SbB", nxt)])
                    cur[0] = nxt

            def P4b(n):
                c = cx[n]
                bo, bok = c["bo"]
                rs, rsk = c["rs"]
                on, onk = scrB.get(512)
                for hl in range(2):
                    stt(on[:, hl * 256:(hl + 1) * 256], bo[:, hl * 256:(hl + 1) * 256], rs[:, hl:hl + 1], glag[:],
                        ALU.mult, ALU.mult, bok + rsk + ["glag"], onk)
                c["on"] = (on, onk)

            def P5(n):
                on, onk = cx[n]["on"]
                tb, tbk = npt()
                for c4 in range(4):
                    tr(tb[:, c4 * 128:(c4 + 1) * 128], on[:, c4 * 128:(c4 + 1) * 128], onk, tbk)
                cp("act", onT[:, hp * 4:(hp + 1) * 4, n * 128:(n + 1) * 128],
                   tb[:, 0:512].rearrange("p (c t) -> p c t", c=4), tbk, [("on", hp * 4 + j, n // 4) for j in range(4)])
                del cx[n]

            yield from pipeline_gen(list(range(ntiles - 1, -1, -1)),
                                    [(5, P4b), (0, P1), (1, P2a), (2, P2b), (3, P3), (4, P4a), (5, P5)], qfill)

    def stage4(hh):
        t0 = hh * 1024
        for jb in range(4):
            def fn(slot, jb=jb):
                for cl in range(2):
                    j = jb * 2 + cl
                    for sl in range(2):
                        s = hh * 2 + sl
                        bank, bk = nps()
                        proj_fm(slot, cl * 128, 128, t0 + sl * 512, 512, bank, bk)
                        sr, srk = scrB.get(512)
                        act(sr, bank[:, 0:512], AF.Silu, bk, srk)
                        o_ = onT[:, j, s * 512:(s + 1) * 512]
                        tt("pool", o_, o_, sr, ALU.mult, [("on", j, s)] + srk, [("on", j, s)])
            items.append((w_in_d[:, C_R + jb * 256:C_R + jb * 256 + 256], fn))

    def stage5(hh):
        t0 = hh * 1024
        th = t0 + 1024 if hh == 0 else t0 - 1
        hcol = 1025 if hh == 0 else 0
        for j in range(8):
            hs = {}

            def fA(slot, j=j, hs=hs):
                for sl in range(2):
                    bank, bk = nps()
                    proj_fm(slot, 0, 128, t0 + sl * 512, 512, bank, bk)
                    cc, cck = scrF.get(512)
                    act(cc, bank[:, 0:512], AF.Copy, bk, cck)
                    hs[("cc", sl)] = (cc, cck)
                if hh == 0:
                    bank, bk = nps()
                    proj_fm(slot, 0, 128, th, 1, bank, bk)
                    cch, cchk = nstat()
                    act(cch, bank[:, 0:1], AF.Copy, bk, cchk)
                    hs["cch"] = (cch, cchk)

            def fB(slot, j=j, hs=hs):
                pv, pk = scrB.get(1026)
                for sl in range(2):
                    bank, bk = nps()
                    proj_fm(slot, 0, 128, t0 + sl * 512, 512, bank, bk)
                    cc, cck = hs[("cc", sl)]
                    tt("dve", pv[:, 1 + sl * 512:1 + (sl + 1) * 512], bank[:, 0:512], cc, ALU.mult, bk + cck, pk)
                if hh == 0:
                    bank, bk = nps()
                    proj_fm(slot, 0, 128, th, 1, bank, bk)
                    cch, cchk = hs["cch"]
                    tt("dve", pv[:, hcol:hcol + 1], bank[:, 0:1], cch, ALU.mult, bk + cchk, pk)
                    cp("pool", pv[:, 0:1], pmeta[:, j:j + 1], ["pmeta"], pk)
                    cp("pool", phalo[:, j:j + 1], pv[:, 1024:1025], pk, [("phalo", j)])
                else:
                    cp("pool", pv[:, 0:1], phalo[:, j:j + 1], [("phalo", j)], pk)
                    mset("pool", pv[:, 1025:1026], 0.0, pk)
                for sl in range(2):
                    c0 = sl * 512
                    tc, tck = scrF.get(512)
                    act(tc, pv[:, 1 + c0:1 + c0 + 512], AF.Copy, pk + ["convw"], tck, scale=convw[:, j, 1:2])
                    stt(tc, pv[:, c0:c0 + 512], convw[:, j, 0:1], tc, ALU.mult, ALU.add, pk + ["convw"] + tck, tck)
                    stt(tc, pv[:, 2 + c0:2 + c0 + 512], convw[:, j, 2:3], tc, ALU.mult, ALU.add, pk + ["convw"] + tck, tck)
                    hs[("tc", sl)] = (tc, tck)

            def fC(slot, j=j, hs=hs):
                for sl in range(2):
                    bank, bk = nps()
                    proj_fm(slot, 0, 128, t0 + sl * 512, 512, bank, bk)
                    sz, szk = scrB.get(512)
                    act(sz, bank[:, 0:512], AF.Silu, bk, szk)
                    hs[("sz", sl)] = (sz, szk)

            def fD(slot, j=j, hs=hs):
                for sl in range(2):
                    bank, bk = nps()
                    proj_fm(slot, 0, 128, t0 + sl * 512, 512, bank, bk)
                    sz, szk = hs[("sz", sl)]
                    tc, tck = hs[("tc", sl)]
                    tb_, tbk_ = scrF.get(512)
                    tt("dve", tb_, bank[:, 0:512], sz, ALU.mult, bk + szk, tbk_)
                    yv, yk = ycT_v(j, sl * 512, 512)
                    tt("pool", yv, tb_, tc, ALU.mult, tbk_ + tck, yk)

            for (c0, f) in ((C_CC, fA), (C_CX, fB), (C_CZ, fC), (C_CB, fD)):
                items.append((w_in_d[:, c0 + j * 128:c0 + (j + 1) * 128], f))

    def stage6(hh):
        t0 = hh * 1024
        for d in range(8):
            st_ = {}

            def fMA(slot, st_=st_):
                for sl in range(2):
                    bank, bk = nps()
                    proj_fm(slot, 0, 128, t0 + sl * 512, 512, bank, bk)
                    sa, sak = scrF.get(512)
                    act(sa, bank[:, 0:512], AF.Sigmoid, bk, sak)
                    st_[("sa", sl)] = (sa, sak)

            def fOC(slot, d=d, st_=st_):
                for sl in range(2):
                    bank, bk = nps()
                    for c in range(8):
                        yv, yk = ycT_v(c, sl * 512, 512)
                        mm(bank[:, 0:512], wbuf[:, slot, c, 0:128], yv, c == 0, c == 7, [("w", slot)] + yk, bk)
                    sa, sak = st_[("sa", sl)]
                    t1, t1k = scrF.get(512)
                    tt("dve", t1, bank[:, 0:512], sa, ALU.mult, bk + sak, t1k)
                    st_[("t1", sl)] = (t1, t1k)

            def fMB(slot, st_=st_):
                for sl in range(2):
                    bank, bk = nps()
                    proj_fm(slot, 0, 128, t0 + sl * 512, 512, bank, bk)
                    sb_, sbk = scrF.get(512)
                    act(sb_, bank[:, 0:512], AF.Sigmoid, bk, sbk)
                    st_[("sb", sl)] = (sb_, sbk)

            def fOG(slot, d=d, st_=st_):
                for sl in range(2):
                    s = hh * 2 + sl
                    bank, bk = nps()
                    for c in range(8):
                        mm(bank[:, 0:512], wbuf[:, slot, c, 0:128], onT[:, c, s * 512:(s + 1) * 512], c == 0, c == 7,
                           [("w", slot), ("on", c, s)], bk)
                    sb_, sbk = st_[("sb", sl)]
                    t2, t2k = scrF.get(512)
                    tt("dve", t2, bank[:, 0:512], sb_, ALU.mult, bk + sbk, t2k)
                    t1, t1k = st_[("t1", sl)]
                    mv, mk = mT_v(d, sl * 512, 512)
                    tt("pool", mv, t1, t2, ALU.add, t1k + t2k, mk)

            items.append((w_in_d[:, C_MA + d * 128:C_MA + (d + 1) * 128], fMA))
            items.append((w_oc_d[:, d * 128:(d + 1) * 128], fOC))
            items.append((w_in_d[:, C_MB + d * 128:C_MB + (d + 1) * 128], fMB))
            items.append((w_og_d[:, d * 128:(d + 1) * 128], fOG))

    def stage7(b, hh):
        def fn(_):
            xs = {}
            if hh == 1 and b + 1 < NB:
                s0pre[b + 1] = stage0_prefetch(lambda i, b=b: x_d[b + 1, i * 128:(i + 1) * 128, :])

            def ldx(il):
                i = hh * 8 + il
                xv16, xk = scrB.get(2048)
                xv = xv16.bitcast(F32)
                dma("sp", xv, x_d[b, i * 128:(i + 1) * 128, :], [], xk, ("xt", "scrB", xk[0][1]))
                xs[il] = (xv, xk)

            ldx(0)
            ldx(1)
            for il in range(8):
                i = hh * 8 + il
                if il + 2 < 8:
                    ldx(il + 2)
                banks = []
                for h2 in range(2):
                    bank, bk = nps()
                    for d in range(8):
                        mv, mk = mT_v(d, il * 128, 128)
                        mm(bank[:, 0:512], mv, wo_view[:, d, h2 * 512:(h2 + 1) * 512], d == 0, d == 7, mk + wo_keys, bk)
                    banks.append((bank, bk))
                ms, msk = nstat(2)
                for h2 in range(2):
                    act(junk[:, 0:512], banks[h2][0][:, 0:512], AF.Square, banks[h2][1], msk, scale=1.0 / 32.0,
                        accum=ms[:, h2:h2 + 1])
                m1, m1k = nstat()
                tt("dve", m1, ms[:, 0:1], ms[:, 1:2], ALU.add, msk, m1k)
                ln, lnk = nstat()
                act(ln, m1, AF.Ln, m1k, lnk, bias=EPS)
                rs, rsk = nstat()
                act(rs, ln, AF.Exp, lnk, rsk, scale=-0.5)
                yv, yk = scrF.get(1024)
                for h2 in range(2):
                    stt(yv[:, h2 * 512:(h2 + 1) * 512], banks[h2][0][:, 0:512], rs, gpost[:, h2 * 512:(h2 + 1) * 512],
                        ALU.mult, ALU.mult, banks[h2][1] + rsk + ["gpost"], yk)
                xv, xk = xs.pop(il)
                tt("pool", yv, yv, xv, ALU.add, yk + xk, yk)
                out_dmas.append(dma("sp", y_d[b, i * 128:(i + 1) * 128, :], yv, yk, [], ("y", yk[0][1])))
        items.append((None, fn))

    def load_wo():
        def fn(_):
            src = w_o_d.rearrange("(ko ki) n -> ki ko n", ki=128)
            for h2 in range(2):
                dma("pool", wo_view[:, :, h2 * 512:(h2 + 1) * 512], src[:, :, h2 * 512:(h2 + 1) * 512], [], wo_keys, "wo")
        items.append((None, fn))

    s0pre[0] = stage0_prefetch(lambda i: x_d[0, i * 128:(i + 1) * 128, :])
    stage0(lambda i: metap_d[:, :], 1)
    for hp in range(2):
        stage1(hp, 1, False, hp == 0)
        passF_meta(hp)
    for jb in range(4):
        hs = {}

        def fA(slot, jb=jb, hs=hs):
            bank, bk = nps()
            for cl in range(2):
                for k in range(8):
                    mm(bank[:, cl * 128:(cl + 1) * 128], wbuf[:, slot, k, cl * 128:(cl + 1) * 128], uT[:, k, 0:128],
                       k == 0, k == 7, [("w", slot), ("uT", 0)], bk)
            cc, cck = scrF.get(256)
            act(cc, bank[:, 0:256], AF.Copy, bk, cck)
            hs["cc"] = (cc, cck)

        def fB(slot, jb=jb, hs=hs):
            bank, bk = nps()
            for cl in range(2):
                for k in range(8):
                    mm(bank[:, cl * 128:(cl + 1) * 128], wbuf[:, slot, k, cl * 128:(cl + 1) * 128], uT[:, k, 0:128],
                       k == 0, k == 7, [("w", slot), ("uT", 0)], bk)
            cc, cck = hs["cc"]
            for cl in range(2):
                j = jb * 2 + cl
                tt("dve", pmeta[:, j:j + 1], bank[:, cl * 128 + 127:cl * 128 + 128], cc[:, cl * 128 + 127:cl * 128 + 128],
                   ALU.mult, bk + cck, ["pmeta"])
        items.append((w_in_d[:, C_CC + jb * 256:C_CC + (jb + 1) * 256], fA))
        items.append((w_in_d[:, C_CX + jb * 256:C_CX + (jb + 1) * 256], fB))
    run_items()

    for b in range(NB):
        stage0(lambda i, b=b: x_d[b, i * 128:(i + 1) * 128, :], 16, pre=s0pre.get(b))
        for hp in range(2):
            stage1(hp, 16, True, hp == 0)
            gla_pair(hp, 16)
        load_wo()
        for hh in range(2):
            stage4(hh)
            stage5(hh)
            stage6(hh)
            stage7(b, hh)
        run_items()

    S.add("sp", None, [], [], extra=out_dmas)

    S.fastv = fastv
    S.finalize()
    sems = {}
    for e in ("pe", "act", "dve", "pool", "sp"):
        sems[("eng", e)] = es.enter_context(nc.semaphore(f"s_{e}"))
    for i, k in enumerate(S.dma_keys):
        sems[("dma", k)] = es.enter_context(nc.semaphore(f"d_{i}"))

    with nc.Block() as block:
        @block.tensor
        def _(e):
            S.emit("pe", e, sems)

        @block.scalar
        def _(e):
            S.emit("act", e, sems)

        @block.vector
        def _(e):
            S.emit("dve", e, sems)

        @block.gpsimd
        def _(e):
            S.emit("pool", e, sems)

        @block.sync
        def _(e):
            S.emit("sp", e, sems)
    es.close()
    return nc, S


def _consts():
    s = np.arange(128)[:, None]
    t = np.arange(128)[None, :]
    g = np.float32(-1.0 / 16.0)
    tri = np.zeros((128, 2, 128), np.float32)
    tri[:, 0, :] = np.where(s <= t, g, 0.0)
    tri[:, 1, :] = np.where(s >= t, g, 0.0)
    trid = np.full((128, 2, 130), g, np.float32)
    trid[:, 0, 0:128] = np.where(s > t, g, 0.0)
    trid[:, 1, 0:128] = np.where(s < t, g, 0.0)
    masks = np.zeros((128, 2, 2, 128), np.float32)
    masks[:, 0, :, :] = np.where(s <= t, 1.0, 0.0)[:, None, :]
    masks[:, 1, :, :] = np.where(s > t, 1.0, 0.0)[:, None, :]
    ident = np.eye(128, dtype=np.float32).astype(ml_dtypes.bfloat16)
    return tri, trid, masks.reshape(128, 512), ident


def make_in_maps(inputs, n_cores, NB):
    f = lambda a: np.ascontiguousarray(np.asarray(a, dtype=np.float32))
    x = f(inputs["x"])
    tri, trid, masks, ident = _consts()
    metap = np.zeros((128, D), np.float32)
    metap[112:128] = f(inputs["meta_tokens"])
    wg = np.zeros((33, 2, 512), np.float32)
    wg[0:16, 0] = f(inputs["w_gate_fwd"])[0]
    wg[32, 0] = f(inputs["b_gate_fwd"])[0]
    wg[16:32, 1] = f(inputs["w_gate_bwd"])[0]
    wg[32, 1] = f(inputs["b_gate_bwd"])[0]
    convw = np.ascontiguousarray(f(inputs["conv_w"])[0].T.reshape(8, 128, 3).transpose(1, 0, 2))
    common = {
        "metap": metap,
        "w_in": f(inputs["w_in"])[0],
        "w_oc": f(inputs["w_out_conv"])[0],
        "w_og": f(inputs["w_out_gla"])[0],
        "w_o": f(inputs["w_merge_out"])[0],
        "gpre_bc": np.ascontiguousarray(np.broadcast_to(f(inputs["norm_pre"])[0][None, :], (128, D))),
        "gpost_bc": np.ascontiguousarray(np.broadcast_to(f(inputs["norm_post"])[0][None, :], (128, D))),
        "glag_bc": np.ascontiguousarray(np.broadcast_to(f(inputs["gla_norm"])[0][None, :], (128, 256))),
        "ident": ident, "tri": tri, "trid": trid, "masks": masks, "wg": wg, "convw": convw,
    }
    maps = []
    for c in range(n_cores):
        m = dict(common)
        m["x"] = np.ascontiguousarray(x[c * NB:(c + 1) * NB])
        maps.append(m)
    return maps


_CACHE = {}


def kernel(**inputs):
    NB = inputs["x"].shape[0] // N_CORES
    if NB not in _CACHE:
        _CACHE[NB] = build_program(NB)[0]
    nc = _CACHE[NB]
    in_maps = make_in_maps(inputs, N_CORES, NB)
    res = run_bass_kernel_spmd(nc, in_maps, core_ids=list(range(N_CORES)))
    out = np.concatenate([np.asarray(r["y"]) for r in res.results], axis=0)
    return out.astype(np.float32)
```
